# Optimizing a Trainium2 kernel written in Bass

```python
import math
import jax, jax.numpy as jnp
from jax import lax
import numpy as np

D_MODEL = 1024
BATCH = 8
SEQ = 8192
DEPTH = 1

GRID_W = 64
CTX_LEN = 256
DA_HEADS = 8
DA_DH = 64
DA_DV = 2 * DA_DH
Q_BLOCK = 128
GLA_HEADS = 4
GLA_DK = D_MODEL // (2 * GLA_HEADS)
GLA_DV = D_MODEL // GLA_HEADS
GLA_GATE_RANK = 16
GLA_GATE_NORM = 16.0
GLA_CHUNK = 64
D_FF = 4 * D_MODEL
ROPE_THETA = 10000.0
ROPE_AXIS_DIM = DA_DH // 2
N_BRANCH = 2
N_MOD = 6
EPS = 1e-6
IN_WIDTHS = (DA_HEADS * 2 * DA_DH,
             DA_HEADS * 2 * DA_DH,
             DA_HEADS * DA_DV,
             GLA_HEADS * GLA_DK,
             GLA_HEADS * GLA_DK,
             GLA_HEADS * GLA_DV,
             2 * GLA_GATE_RANK,
             GLA_HEADS * GLA_DV,
             N_BRANCH * D_MODEL)
D_IN = sum(IN_WIDTHS)

kernel_name = 'hybrid_diffattn_gla_dit_block'


def rms_norm(x, gain):
    xf = x.astype(jnp.float32)
    y = xf * lax.rsqrt(jnp.mean(xf * xf, axis=-1, keepdims=True) + EPS)
    return (y * gain.astype(jnp.float32)).astype(x.dtype)


def modulate(x, shift, scale):
    return x * (1.0 + scale) + shift


def axial_angles(T):
    rows = T // GRID_W
    r, col = jnp.meshgrid(jnp.arange(rows), jnp.arange(GRID_W), indexing='ij')
    inv = ROPE_THETA ** (-jnp.arange(0, ROPE_AXIS_DIM, 2, dtype=jnp.float32) / ROPE_AXIS_DIM)
    ang_r = r.reshape(-1, 1).astype(jnp.float32) * inv
    ang_c = col.reshape(-1, 1).astype(jnp.float32) * inv
    return ang_r, ang_c


def rotate_half_pairs(x, ang):
    cos = jnp.cos(ang)[:, None, None, :].astype(x.dtype)
    sin = jnp.sin(ang)[:, None, None, :].astype(x.dtype)
    x1, x2 = jnp.split(x, 2, axis=-1)
    return jnp.concatenate([x1 * cos - x2 * sin, x2 * cos + x1 * sin], axis=-1)


def axial_rope(x, ang_r, ang_c):
    return jnp.concatenate([rotate_half_pairs(x[..., :ROPE_AXIS_DIM], ang_r),
                            rotate_half_pairs(x[..., ROPE_AXIS_DIM:], ang_c)], axis=-1)


def diff_attend(q, k, v, lam):
    B, Tq = q.shape[:2]
    nblk = Tq // Q_BLOCK
    qb = q.reshape(B, nblk, Q_BLOCK, DA_HEADS, 2, DA_DH).swapaxes(0, 1)
    scale = DA_DH ** -0.5

    def block(qi):
        s = jnp.einsum('bqhcd,bkhcd->bhcqk', qi, k).astype(jnp.float32) * scale
        p = jax.nn.softmax(s, axis=-1)
        p = p[:, :, 0] - lam * p[:, :, 1]
        return jnp.einsum('bhqk,bkhe->bqhe', p.astype(v.dtype), v)

    o = lax.map(block, qb)
    return o.swapaxes(0, 1).reshape(B, Tq, DA_HEADS, DA_DV)


def gla_chunk_scan(q, k, v, g, state):
    B, H, T, _ = q.shape
    dv = v.shape[-1]
    n = T // GLA_CHUNK

    def chunks(a):
        return jnp.moveaxis(a.astype(jnp.float32).reshape(B, H, n, GLA_CHUNK, a.shape[-1]), 2, 0)

    tri = jnp.tril(jnp.ones((GLA_CHUNK, GLA_CHUNK), dtype=bool))[:, :, None]

    def step(S, inp):
        qc, kc, vc, gc = inp
        b = jnp.cumsum(gc, axis=2)
        o_inter = jnp.einsum('bhik,bhkv->bhiv', qc * jnp.exp(b), S)
        diff = b[:, :, :, None, :] - b[:, :, None, :, :]
        decay = jnp.where(tri, jnp.exp(jnp.where(tri, diff, 0.0)), 0.0)
        A = jnp.einsum('bhik,bhjk,bhijk->bhij', qc, kc, decay)
        o_intra = jnp.einsum('bhij,bhjv->bhiv', A, vc)
        b_last = b[:, :, -1:]
        S = jnp.exp(b_last)[:, :, 0, :, None] * S + jnp.einsum(
            'bhjk,bhjv->bhkv', kc * jnp.exp(b_last - b), vc)
        return S, o_inter + o_intra

    S, o = lax.scan(step, state.astype(jnp.float32), (chunks(q), chunks(k), chunks(v), chunks(g)))
    return jnp.moveaxis(o, 0, 2).reshape(B, H, T, dv), S


def gla_final_state(k, v, g):
    b = jnp.cumsum(g.astype(jnp.float32), axis=2)
    w = jnp.exp(b[:, :, -1:] - b)
    return jnp.einsum('bhtk,bhtv->bhkv', k.astype(jnp.float32) * w, v.astype(jnp.float32))


def flip_t(a):
    return jnp.flip(a, axis=2)


def gla_bidir(q, k, v, g_f, g_b, S_f, S_b):
    o_f, _ = gla_chunk_scan(q, k, v, g_f, S_f)
    o_b, _ = gla_chunk_scan(flip_t(q), flip_t(k), flip_t(v), flip_t(g_b), S_b)
    return o_f + flip_t(o_b)


def project(h, w_in, w_gate_up, b_gate_up):
    B, T = h.shape[:2]
    parts = jnp.split(h @ w_in, np.cumsum(IN_WIDTHS)[:-1].tolist(), axis=-1)
    da_q, da_k, da_v, gq, gk, gv, g_low, g_out, merge = parts
    da_q = da_q.reshape(B, T, DA_HEADS, 2, DA_DH)
    da_k = da_k.reshape(B, T, DA_HEADS, 2, DA_DH)
    da_v = da_v.reshape(B, T, DA_HEADS, DA_DV)

    def heads(a, d):
        return a.reshape(B, T, GLA_HEADS, d).transpose(0, 2, 1, 3)

    gq = heads(gq, GLA_DK) * (GLA_DK ** -0.5)
    gk = heads(gk, GLA_DK)
    gv = heads(gv, GLA_DV)
    gate = jnp.einsum('btzr,zrk->btzk', g_low.reshape(B, T, 2, GLA_GATE_RANK), w_gate_up) + b_gate_up
    gate = jax.nn.log_sigmoid(gate.astype(jnp.float32)) / GLA_GATE_NORM
    g_f = heads(gate[:, :, 0], GLA_DK)
    g_b = heads(gate[:, :, 1], GLA_DK)
    return da_q, da_k, da_v, gq, gk, gv, g_f, g_b, g_out, merge


def finish_and_merge(o_da, o_gla, g_out, merge, p, lam_init):
    B, T = o_da.shape[:2]
    y_da = (rms_norm(o_da, p['da_head_norm']) * (1.0 - lam_init)).reshape(B, T, DA_HEADS * DA_DV)
    o_gla = jnp.swapaxes(o_gla, 1, 2).astype(g_out.dtype)
    y_gla = (rms_norm(o_gla, p['gla_head_norm'])
             * jax.nn.silu(g_out.reshape(B, T, GLA_HEADS, GLA_DV))).reshape(B, T, GLA_HEADS * GLA_DV)
    gates = jax.nn.sigmoid(merge).reshape(B, T, N_BRANCH, D_MODEL)
    y = gates[:, :, 0] * (y_da @ p['w_branch_da']) + gates[:, :, 1] * (y_gla @ p['w_branch_gla'])
    return y @ p['w_out']


def sq_relu_mlp(h, w_ff1, w_ff2):
    return jnp.square(jax.nn.relu(h @ w_ff1)) @ w_ff2


def trunk_layer(x, ctx, mod_lat, mod_ctx, p, layer_idx, update_ctx):
    T = x.shape[1]
    lam_init = 0.8 - 0.6 * math.exp(-0.3 * layer_idx)
    f32 = jnp.float32
    lam = (jnp.exp(jnp.sum(p['lambda_q1'].astype(f32) * p['lambda_k1'].astype(f32)))
           - jnp.exp(jnp.sum(p['lambda_q2'].astype(f32) * p['lambda_k2'].astype(f32))) + lam_init)
    sh1, sc1, gt1, sh2, sc2, gt2 = jnp.split(mod_lat, N_MOD, axis=-1)
    csh1, csc1, cgt1, csh2, csc2, cgt2 = jnp.split(mod_ctx, N_MOD, axis=-1)

    h = modulate(rms_norm(x, p['pre_norm1']), sh1, sc1)
    hc = modulate(rms_norm(ctx, p['pre_norm1']), csh1, csc1)
    da_q, da_k, da_v, gq, gk, gv, g_f, g_b, g_out, merge = project(h, p['w_in'], p['w_gate_up'], p['b_gate_up'])
    cda_q, cda_k, cda_v, cgq, cgk, cgv, cg_f, cg_b, cg_out, cmerge = project(hc, p['w_in'], p['w_gate_up'], p['b_gate_up'])

    ang_r, ang_c = axial_angles(T)
    q_lat = axial_rope(da_q, ang_r, ang_c)
    k_lat = axial_rope(da_k, ang_r, ang_c)
    k_all = jnp.concatenate([k_lat, cda_k], axis=1)
    v_all = jnp.concatenate([da_v, cda_v], axis=1)
    o_da = diff_attend(q_lat, k_all, v_all, lam)

    S_f = gla_final_state(cgk, cgv, cg_f)
    S_b = gla_final_state(flip_t(cgk), flip_t(cgv), flip_t(cg_b))
    o_gla = gla_bidir(gq, gk, gv, g_f, g_b, S_f, S_b)

    y = finish_and_merge(o_da, o_gla, g_out, merge, p, lam_init)
    x_new = x + gt1 * rms_norm(y, p['post_norm1'])

    h2 = modulate(rms_norm(x_new, p['pre_norm2']), sh2, sc2)
    x_new = x_new + gt2 * rms_norm(sq_relu_mlp(h2, p['w_ff1'], p['w_ff2']), p['post_norm2'])

    if update_ctx:
        o_da_c = diff_attend(cda_q, cda_k, cda_v, lam)
        zero_state = jnp.zeros_like(S_f)
        o_gla_c = gla_bidir(cgq, cgk, cgv, cg_f, cg_b, zero_state, zero_state)
        yc = finish_and_merge(o_da_c, o_gla_c, cg_out, cmerge, p, lam_init)
        ctx = ctx + cgt1 * rms_norm(yc, p['post_norm1'])
        hc2 = modulate(rms_norm(ctx, p['pre_norm2']), csh2, csc2)
        ctx = ctx + cgt2 * rms_norm(sq_relu_mlp(hc2, p['w_ff1'], p['w_ff2']), p['post_norm2'])
    return x_new, ctx


def setup_inputs(seed: int = 0) -> dict:
    key = jax.random.key(seed)
    ks = jax.random.split(key, 26)
    D = D_MODEL
    nrm = jax.random.normal
    f32 = jnp.float32

    def gain(k, d):
        return 1.0 + 0.05 * nrm(k, (DEPTH, d), f32)

    return {
        'x': nrm(ks[0], (BATCH, SEQ, D), f32),
        'c': nrm(ks[1], (BATCH, D), f32),
        'ctx': nrm(ks[2], (BATCH, CTX_LEN, D), f32),
        'c_ctx': nrm(ks[3], (D,), f32),
        'w_mod': nrm(ks[4], (DEPTH, D, N_MOD * D), f32) * (0.5 * D ** -0.5),
        'b_mod': 0.01 * nrm(ks[5], (DEPTH, N_MOD * D), f32),
        'pre_norm1': gain(ks[6], D),
        'w_in': nrm(ks[7], (DEPTH, D, D_IN), f32) * D ** -0.5,
        'w_gate_up': nrm(ks[8], (DEPTH, 2, GLA_GATE_RANK, GLA_HEADS * GLA_DK), f32) * GLA_GATE_RANK ** -0.5,
        'b_gate_up': 0.1 * nrm(ks[9], (DEPTH, 2, GLA_HEADS * GLA_DK), f32),
        'lambda_q1': 0.1 * nrm(ks[10], (DEPTH, DA_DH), f32),
        'lambda_k1': 0.1 * nrm(ks[11], (DEPTH, DA_DH), f32),
        'lambda_q2': 0.1 * nrm(ks[12], (DEPTH, DA_DH), f32),
        'lambda_k2': 0.1 * nrm(ks[13], (DEPTH, DA_DH), f32),
        'da_head_norm': gain(ks[14], DA_DV),
        'gla_head_norm': gain(ks[15], GLA_DV),
        'w_branch_da': nrm(ks[16], (DEPTH, DA_HEADS * DA_DV, D), f32) * (DA_HEADS * DA_DV) ** -0.5,
        'w_branch_gla': nrm(ks[17], (DEPTH, GLA_HEADS * GLA_DV, D), f32) * (GLA_HEADS * GLA_DV) ** -0.5,
        'w_out': nrm(ks[18], (DEPTH, D, D), f32) * D ** -0.5,
        'post_norm1': gain(ks[19], D),
        'pre_norm2': gain(ks[20], D),
        'w_ff1': nrm(ks[21], (DEPTH, D, D_FF), f32) * D ** -0.5,
        'w_ff2': nrm(ks[22], (DEPTH, D_FF, D), f32) * D_FF ** -0.5,
        'post_norm2': gain(ks[23], D),
    }


def reference(x, c, ctx, c_ctx, w_mod, b_mod, pre_norm1, w_in, w_gate_up, b_gate_up,
              lambda_q1, lambda_k1, lambda_q2, lambda_k2, da_head_norm, gla_head_norm,
              w_branch_da, w_branch_gla, w_out, post_norm1, pre_norm2, w_ff1, w_ff2, post_norm2):
    for layer in range(DEPTH):
        mod_lat = (jax.nn.silu(c) @ w_mod[layer] + b_mod[layer])[:, None, :]
        mod_ctx = (jax.nn.silu(c_ctx) @ w_mod[layer] + b_mod[layer])[None, None, :]
        p = {
            'pre_norm1': pre_norm1[layer], 'w_in': w_in[layer],
            'w_gate_up': w_gate_up[layer], 'b_gate_up': b_gate_up[layer],
            'lambda_q1': lambda_q1[layer], 'lambda_k1': lambda_k1[layer],
            'lambda_q2': lambda_q2[layer], 'lambda_k2': lambda_k2[layer],
            'da_head_norm': da_head_norm[layer], 'gla_head_norm': gla_head_norm[layer],
            'w_branch_da': w_branch_da[layer], 'w_branch_gla': w_branch_gla[layer],
            'w_out': w_out[layer], 'post_norm1': post_norm1[layer],
            'pre_norm2': pre_norm2[layer], 'w_ff1': w_ff1[layer], 'w_ff2': w_ff2[layer],
            'post_norm2': post_norm2[layer],
        }
        x, ctx = trunk_layer(x, ctx, mod_lat, mod_ctx, p, layer, layer + 1 < DEPTH)
    return x
```

```python
import math
from contextlib import ExitStack
import numpy as np
import ml_dtypes
import concourse.bass as bass
import concourse.mybir as mybir
from concourse.bass_utils import run_bass_kernel_spmd

F32 = mybir.dt.float32
BF16 = mybir.dt.bfloat16
AF = mybir.ActivationFunctionType
ALU = mybir.AluOpType

T = 8192
CT = 256
TT = T + CT
D = 1024
DFF = 4096
EPS = 1e-6
LAM_INIT = 0.2
NCORES = 8
ARENA_WORDS = 49152

ENGS = ("pe", "act", "dve", "pool", "sp")
SAME_ENGINE_SYNC = True


class _Op:
    __slots__ = ("eng", "fn", "reads", "writes", "dma_key", "deps", "signal",
                 "sem", "ticket", "idx", "is_dma", "waits")


class Prog:
    def __init__(self, nc):
        self.nc = nc
        self.ops = []
        self.last_writer = {}
        self.readers = {}
        self.phase_last = {}
        self.phase_dmas = []
        self.pending_bar = {}
        self.phase_idx = 0
        self.op_phase = []

    def add(self, eng, fn, reads=(), writes=(), dma_key=None):
        op = _Op()
        op.eng = eng
        op.fn = fn
        op.reads = tuple(reads)
        op.writes = tuple(writes)
        op.dma_key = dma_key
        op.is_dma = dma_key is not None
        op.idx = len(self.ops)
        op.signal = op.is_dma
        deps = set()
        lw = self.last_writer
        rd = self.readers
        for k in op.reads:
            w = lw.get(k)
            if w is not None:
                deps.add(w)
            if type(k) is tuple and k[0] == "ps":
                r = rd.get(k)
                if r:
                    for e_, i_ in r[0].items():
                        if e_ != eng:
                            deps.add(i_)
        for k in op.writes:
            w = lw.get(k)
            if w is not None:
                deps.add(w)
            r = rd.get(k)
            if r:
                deps.update(r[0].values())
                deps.update(r[1])
        if eng in self.pending_bar:
            deps.update(self.pending_bar.pop(eng))
        deps.discard(op.idx)
        op.deps = deps
        for k in op.reads:
            r = rd.get(k)
            if r is None:
                r = rd[k] = ({}, [])
            if op.is_dma:
                r[1].append(op.idx)
            else:
                r[0][eng] = op.idx
        for k in op.writes:
            lw[k] = op.idx
            rd[k] = ({}, [])
        self.ops.append(op)
        self.op_phase.append(self.phase_idx)
        if op.is_dma:
            self.phase_dmas.append(op.idx)
        else:
            self.phase_last[eng] = op.idx
        return op

    def barrier(self):
        deps = set(self.phase_last.values()) | set(self.phase_dmas)
        for e in ENGS:
            s = self.pending_bar.get(e)
            if s is None:
                self.pending_bar[e] = set(deps)
            else:
                s.update(deps)
        self.phase_dmas = []
        self.phase_idx += 1

    def finalize(self, block, new_sem):
        ops = self.ops
        for op in ops:
            keep = set()
            for d in op.deps:
                p = ops[d]
                if (not p.is_dma) and (not op.is_dma) and p.eng == op.eng:
                    if p.eng == "pe" or not SAME_ENGINE_SYNC:
                        continue
                keep.add(d)
                p.signal = True
            op.deps = keep
        eng_sem = {e: new_sem("s_" + e) for e in ENGS}
        dma_sem = {False: {}, True: {}}
        sem_lists = {False: [], True: []}
        cur_phase = -1
        counts = {}
        for op in ops:
            if not op.signal:
                continue
            if op.is_dma:
                if self.op_phase[op.idx] != cur_phase:
                    cur_phase = self.op_phase[op.idx]
                    dma_sem = {False: {}, True: {}}
                sw = op.eng == "pool"
                dsm = dma_sem[sw]
                sem_list = sem_lists[sw]
                s = dsm.get(op.dma_key)
                if s is None:
                    if len(dsm) >= len(sem_list):
                        sem_list.append(new_sem("d%s_%d" % ("s" if sw else "h", len(sem_list))))
                    s = dsm[op.dma_key] = sem_list[len(dsm)]
                op.sem = s
                counts[id(s)] = counts.get(id(s), 0) + 16
                op.ticket = counts[id(s)]
            else:
                s = eng_sem[op.eng]
                op.sem = s
                counts[id(s)] = counts.get(id(s), 0) + 1
                op.ticket = counts[id(s)]
        self.n_dma_sems = len(sem_lists[False]) + len(sem_lists[True])
        waited = {e: {} for e in ENGS}
        for op in ops:
            need = {}
            for d in op.deps:
                p = ops[d]
                key = id(p.sem)
                cur = need.get(key)
                if cur is None or cur[1] < p.ticket:
                    need[key] = (p.sem, p.ticket)
            w = waited[op.eng]
            lst = []
            for key, (s, v) in need.items():
                if w.get(key, 0) >= v:
                    continue
                w[key] = v
                lst.append((s, v))
            op.waits = lst
        per_eng = {e: [] for e in ENGS}
        for op in ops:
            per_eng[op.eng].append(op)

        def emit(e, engobj):
            for op in per_eng[e]:
                for (s, v) in op.waits:
                    engobj.wait_ge(s, v)
                inst = op.fn(engobj)
                if op.signal:
                    inst.then_inc(op.sem, 16 if op.is_dma else 1)

        block.sync(lambda e: emit("sp", e))
        block.tensor(lambda e: emit("pe", e))
        block.scalar(lambda e: emit("act", e))
        block.vector(lambda e: emit("dve", e))
        block.gpsimd(lambda e: emit("pool", e))


class Arena:
    def __init__(self, ap, base, limit):
        self.ap = ap
        self.base = base
        self.off = base
        self.limit = limit

    def reset(self):
        self.off = self.base

    def f32(self, n):
        n = (n + 1) // 2 * 2
        a = self.ap[:, self.off:self.off + n]
        self.off += n
        assert self.off <= self.limit, ("arena overflow", self.off, self.limit)
        return a

    def b16(self, n):
        w = (n + 3) // 4 * 2
        a = self.ap[:, self.off:self.off + w].bitcast(BF16)
        self.off += w
        assert self.off <= self.limit, ("arena overflow", self.off, self.limit)
        return a[:, 0:n]


class Slots:
    def __init__(self, name, aps):
        self.name = name
        self.aps = aps
        self.i = 0

    def next(self):
        k = self.i % len(self.aps)
        self.i += 1
        return self.aps[k], (self.name, k)


_UID = [0]


def uid(p):
    _UID[0] += 1
    return "%s%d" % (p, _UID[0])


def build_program(debug=False):
    nc = bass.Bass("TRN2", target_bir_lowering=False)

    def din(name, shape, dt=F32):
        return nc.dram_tensor(name, shape, dt, kind="ExternalInput").ap()

    def dscr(name, shape, dt):
        kind = "ExternalOutput" if (debug and name in DEBUG_OUT) else "Internal"
        return nc.dram_tensor(name, shape, dt, kind=kind).ap()

    x = din("x", [T, D])
    ctx = din("ctx", [CT, D])
    cvec = din("cvec", [16, 128])
    w_mod = din("w_mod", [D, 6 * D])
    b_mod = din("b_mod", [1, 6 * D])
    norms = din("norms", [4, D])
    w_in = din("w_in", [D, 8224])
    wgu = din("wgu", [2, 17, 512])
    lambdas = din("lambdas", [1, 256])
    hn_da = din("hn_da", [1, 128])
    hn_gla = din("hn_gla", [1, 256])
    w_bda = din("w_bda", [D, D])
    w_bgla = din("w_bgla", [D, D])
    w_out = din("w_out", [D, D])
    w_ff1 = din("w_ff1", [D, DFF])
    w_ff2 = din("w_ff2", [DFF, D])
    ropecos = din("ropecos", [128, T])
    ropesin = din("ropesin", [128, T])
    out = nc.dram_tensor("out", [T, D], F32, kind="ExternalOutput").ap()

    modv = dscr("modv", [8, D], F32)
    hT = dscr("hT", [128, 8, TT], BF16)
    qT = dscr("qT", [8, 128, T], BF16)
    kT = dscr("kT", [8, 128, TT], BF16)
    vtok = dscr("vtok", [TT, D], BF16)
    QF = dscr("QF", [4, 128, TT], BF16)
    KF = dscr("KF", [4, 128, TT], BF16)
    QB = dscr("QB", [4, 128, TT], BF16)
    KB = dscr("KB", [4, 128, TT], BF16)
    KHF = dscr("KHF", [TT, 512], BF16)
    KHB = dscr("KHB", [TT, 512], BF16)
    GV = dscr("GV", [TT, D], BF16)
    gout = dscr("gout", [T, D], BF16)
    gatesT = dscr("gatesT", [16, 128, T], BF16)
    ydaT = dscr("ydaT", [8, 128, T], BF16)
    o_f = dscr("o_f", [T, D], F32)
    yglaT = dscr("yglaT", [8, 128, T], BF16)
    xnew = dscr("xnew", [T, D], F32)
    h2T = dscr("h2T", [128, 8, T], BF16)

    st = ExitStack()
    with st:
        ar = st.enter_context(nc.sbuf_tensor("arena", [128, ARENA_WORDS], F32))
        ps = st.enter_context(nc.psum_tensor("ps", [128, 4096], F32))
        block = st.enter_context(nc.Block())
        P = Prog(nc)

        def bank(i):
            return ps[:, i * 512:(i + 1) * 512]

        psrr = [0]

        def next_bank():
            i = psrr[0] % 8
            psrr[0] += 1
            return bank(i), ("ps", i), i

        def next_pair():
            i = ((psrr[0] + 1) // 2) % 4
            psrr[0] = ((psrr[0] + 1) // 2) * 2 + 2
            return ps[:, i * 1024:(i + 1) * 1024], [("ps", 2 * i), ("ps", 2 * i + 1)], i

        def dma(eng, out_, in_, reads, writes, key):
            P.add(eng, lambda e: e.dma_start(out=out_, in_=in_), reads=reads, writes=writes, dma_key=key)

        def mm(out_, lhsT, rhs, start, stop, reads, writes, **kw):
            P.add("pe", lambda e: e.matmul(out_, lhsT, rhs, start=start, stop=stop, **kw), reads=reads, writes=writes)

        def act(out_, in_, func, reads, writes, **kw):
            P.add("act", lambda e: e.activation(out=out_, in_=in_, func=func, **kw), reads=reads, writes=writes)

        def tt(eng, out_, in0, in1, op, reads, writes):
            P.add(eng, lambda e: e.tensor_tensor(out=out_, in0=in0, in1=in1, op=op), reads=reads, writes=writes)

        def ts(eng, out_, in0, s1, s2, op0, op1, reads, writes):
            P.add(eng, lambda e: e.tensor_scalar(out=out_, in0=in0, scalar1=s1, scalar2=s2, op0=op0, op1=op1),
                  reads=reads, writes=writes)

        def stt(out_, in0, scalar, in1, op0, op1, reads, writes, **kw):
            P.add("dve", lambda e: e.scalar_tensor_tensor(out=out_, in0=in0, scalar=scalar, in1=in1, op0=op0, op1=op1, **kw),
                  reads=reads, writes=writes)

        def cp(eng, out_, in_, reads, writes):
            if eng == "act":
                P.add("act", lambda e: e.activation(out=out_, in_=in_, func=AF.Copy), reads=reads, writes=writes)
            else:
                P.add(eng, lambda e: e.tensor_copy(out=out_, in_=in_), reads=reads, writes=writes)

        def memset(eng, ap, val, writes):
            P.add(eng, lambda e: e.memset(ap, val), writes=writes)

        A = Arena(ar, 0, ARENA_WORDS)
        identf = A.f32(128)
        ident = A.b16(128)
        ones_row = A.f32(128)
        lamcol = A.f32(2)
        hnda = A.f32(128)
        hngla = A.f32(256)
        maskF = A.b16(512)
        maskB = A.b16(512)
        c_eps = A.f32(2)
        c_nhalf = A.f32(4)
        c_one = A.f32(2)
        DECF = A.f32(66 * 4)
        DECB = A.f32(66 * 4)
        zer = A.f32(512)
        A.base = A.off

        memset("pool", zer, 0.0, ["zer"])
        P.add("pool", lambda e: e.affine_select(out=identf, in_=zer[:, 0:128], pattern=[[-1, 128]], compare_op=ALU.not_equal,
                                                fill=1.0, base=0, channel_multiplier=1), reads=["zer"], writes=["identf"])
        cp("dve", ident, identf, ["identf"], ["ident"])
        memset("pool", ones_row, 1.0, ["ones_row"])
        memset("pool", c_eps, EPS, ["c_eps"])
        memset("pool", c_nhalf, -0.5, ["c_nhalf"])
        memset("pool", c_one, 1.0, ["c_one"])
        zer3 = zer.rearrange("p (h t) -> p h t", t=128)
        maskF3 = maskF.rearrange("p (h t) -> p h t", t=128)
        maskB3 = maskB.rearrange("p (h t) -> p h t", t=128)
        P.add("pool", lambda e: e.affine_select(out=maskF3, in_=zer3, pattern=[[0, 4], [-1, 128]], compare_op=ALU.is_gt,
                                                fill=1.0, base=0, channel_multiplier=1), reads=["zer"], writes=["maskF"])
        P.add("pool", lambda e: e.affine_select(out=maskB3, in_=zer3, pattern=[[0, 4], [1, 128]], compare_op=ALU.is_gt,
                                                fill=1.0, base=0, channel_multiplier=-1), reads=["zer"], writes=["maskB"])
        Psw = A.b16(128)
        pswf = A.f32(128)
        memset("pool", pswf, 0.0, ["pswf"])
        pswf3 = pswf.rearrange("p (b j) -> p b j", j=32)
        zer16 = zer[:, 0:64].rearrange("p (b j) -> p b j", j=16)
        P.add("pool", lambda e: e.affine_select(out=pswf3[:, :, 16:32], in_=zer16, pattern=[[32, 4], [1, 16]], compare_op=ALU.not_equal,
                                                fill=1.0, base=0, channel_multiplier=-1), reads=["zer", "pswf"], writes=["pswf"])
        P.add("pool", lambda e: e.affine_select(out=pswf3[:, :, 0:16], in_=zer16, pattern=[[32, 4], [1, 16]], compare_op=ALU.not_equal,
                                                fill=1.0, base=16, channel_multiplier=-1), reads=["zer", "pswf"], writes=["pswf"])
        cp("dve", Psw, pswf, ["pswf"], ["Psw"])
        A.base = A.off
        CONST_KEYS = ["identf", "ident", "ones_row", "c_eps", "c_nhalf", "c_one", "maskF", "maskB"]

        def rstd_from_ssq(ssq, n, inv_n, rk, wk):
            ts("pool", ssq, ssq, inv_n, EPS, ALU.mult, ALU.add, rk, wk)
            tt("pool", ssq, ssq, c_nhalf[:, 0:n], ALU.pow, list(wk) + ["c_nhalf"], wk)

        def phase0():
            A.reset()
            cv = A.f32(128)
            scT = A.f32(16)
            sc2 = A.f32(16)
            modrow = A.f32(6 * D)
            bm = A.f32(6 * D)
            nrm = A.f32(4 * D)
            res = modrow
            wm = [A.f32(8 * 512) for _ in range(2)]
            lamt = A.f32(256)
            lamp = A.f32(128)
            lams = A.f32(4)
            ones2 = ones_row[0:1, 0:2]
            dma("sp", cv[0:16, :], cvec, [], ["cv"], "p0_cv")
            dma("sp", bm[0:1, :], b_mod, [], ["bm"], "p0_bm")
            dma("sp", nrm[0:2, :].rearrange("p (a d) -> p a d", d=D), norms.partition_broadcast(2), [], ["nrm"], "p0_nrm")
            memset("pool", lams, 0.0, ["lams"])
            dma("sp", lamt[0:1, :], lambdas, [], ["lamt"], "p0_lam")
            dma("sp", hnda, hn_da[0, :].partition_broadcast(128), [], ["hnda_raw"], "p0_hnda")
            dma("sp", hngla, hn_gla[0, :].partition_broadcast(128), [], ["hngla"], "p0_hngla")
            ts("dve", hnda, hnda, 1.0 - LAM_INIT, None, ALU.mult, ALU.bypass, ["hnda_raw"], ["hnda"])
            act(cv[0:16, :], cv[0:16, :], AF.Silu, ["cv"], ["cv"])
            b0, k0, _ = next_bank()
            P.add("pe", lambda e: e.transpose(b0[:, 0:16], cv[0:16, :], identf[0:16, 0:16]), reads=["cv", "identf"], writes=[k0])
            cp("dve", scT, b0[:, 0:16], [k0], ["scT"])
            sc2v = sc2.rearrange("p (k v) -> p k v", v=2)
            cp("dve", sc2v[:, :, 0], scT[:, 0:8], ["scT"], ["sc2a"])
            cp("dve", sc2v[:, :, 1], scT[:, 8:16], ["scT"], ["sc2b"])
            for blk in range(12):
                wt = wm[blk % 2]
                wk = ("wm", blk % 2)
                dma("sp", wt.rearrange("p (k n) -> p k n", n=512),
                    w_mod[:, blk * 512:(blk + 1) * 512].rearrange("(k p) n -> p k n", p=128), [], [wk], "p0_wm%d" % (blk % 2))
                b1, k1, _ = next_bank()
                for k in range(8):
                    mm(b1[0:2, :], sc2v[:, k, :], wt[:, k * 512:(k + 1) * 512], k == 0, False,
                       ["sc2a", "sc2b", wk], [k1])
                mm(b1[0:2, :], ones2, bm[0:1, blk * 512:(blk + 1) * 512], False, True, ["ones_row", "bm"], [k1])
                cp("dve", modrow[0:2, blk * 512:(blk + 1) * 512], b1[0:2, :], [k1], [("modrow", blk // 2)])
            mr = lambda j: modrow[0:2, j * D:(j + 1) * D]
            nr = lambda j: nrm[0:2, j * D:(j + 1) * D]
            rs = lambda j: res[0:2, j * D:(j + 1) * D]
            mk = lambda j: ("modrow", j)
            stt(rs(1), mr(1), 1.0, nr(0), ALU.add, ALU.mult, [mk(1), "nrm"], [mk(1)])
            tt("dve", rs(2), mr(2), nr(1), ALU.mult, [mk(2), "nrm"], [mk(2)])
            stt(rs(4), mr(4), 1.0, nr(2), ALU.add, ALU.mult, [mk(4), "nrm"], [mk(4)])
            tt("dve", rs(5), mr(5), nr(3), ALU.mult, [mk(5), "nrm"], [mk(5)])
            rk = [mk(j) for j in range(6)]
            dma("sp", modv[0:6, :], res[0:1, :].rearrange("p (a d) -> p a d", d=D), rk, ["modv"], "p0_mv0")
            dma("sp", modv[6:8, :], res[1:2, 0:2 * D].rearrange("p (a d) -> p a d", d=D), rk, ["modv2"], "p0_mv1")
            l4 = lamt[0:1, :].rearrange("p (a b d) -> p a b d", b=2, d=64)
            lp = lamp[0:1, :].rearrange("p (a d) -> p a d", d=64)
            tt("dve", lp, l4[:, :, 0, :], l4[:, :, 1, :], ALU.mult, ["lamt"], ["lamp"])
            P.add("dve", lambda e: e.tensor_reduce(out=lams[0:1, 0:2], in_=lp, op=ALU.add, axis=mybir.AxisListType.X),
                  reads=["lamp", "lams"], writes=["lams"])
            act(lams[0:1, 0:2], lams[0:1, 0:2], AF.Exp, ["lams"], ["lams"])
            tt("dve", lams[0:1, 2:3], lams[0:1, 0:1], lams[0:1, 1:2], ALU.subtract, ["lams"], ["lams2"])
            ts("dve", lams[0:1, 2:3], lams[0:1, 2:3], LAM_INIT, None, ALU.add, ALU.bypass, ["lams2"], ["lams2"])
            b2, k2, _ = next_bank()
            mm(b2[:, 0:2], ones_row[0:1, :], lams[0:1, 2:4], True, True, ["ones_row", "lams2"], [k2])
            cp("dve", lamcol[:, 0:1], b2[:, 0:1], [k2], ["lamcol"])

        def load_rows(dst, rows, key, dkey):
            dma("sp", dst, modv[rows, :].partition_broadcast(128), ["modv", "modv2"], [key], dkey)

        def phaseH():
            G1 = A.f32(D); sh1 = A.f32(D); cG1 = A.f32(D); csh1 = A.f32(D)
            load_rows(sh1, 0, "sh1", "ph_r0")
            load_rows(G1, 1, "G1", "ph_r1")
            load_rows(csh1, 6, "csh1", "ph_r2")
            load_rows(cG1, 7, "cG1", "ph_r3")
            xts = Slots("xt", [A.f32(D) for _ in range(4)])
            junk = A.b16(D)
            hns = Slots("hn", [A.f32(D) for _ in range(2)])
            hbs = Slots("hb", [A.b16(D) for _ in range(3)])
            hTs = Slots("hTt", [A.b16(8 * 512) for _ in range(2)])
            sqs = Slots("ssq", [A.f32(2) for _ in range(4)])
            pendB = []

            def stageB():
                for (hb, hbk, hT3, hTk, s) in pendB:
                    bk, bkk, _ = next_bank()
                    b16v = bk.bitcast(BF16)
                    for k in range(8):
                        P.add("pe", lambda e, k=k, b16v=b16v, hb=hb: e.transpose(b16v[:, k * 128:(k + 1) * 128],
                                                                                 hb[:, k * 128:(k + 1) * 128], ident),
                              reads=[hbk, "ident"], writes=[bkk])
                    cp("act" if s % 2 == 0 else "dve", hT3[:, :, s * 128:(s + 1) * 128],
                       b16v.rearrange("p (k t) -> p k t", t=128), [bkk], [(hTk, s)])
                del pendB[:]

            pendStore = []
            for g in range(17):
                nsub = 4 if g < 16 else 2
                hTt, hTk = hTs.next()
                hT3 = hTt.rearrange("p (k t) -> p k t", t=512)
                for s in range(nsub):
                    src = x[g * 512 + s * 128: g * 512 + (s + 1) * 128, :] if g < 16 else ctx[s * 128:(s + 1) * 128, :]
                    Gk, Sk = ("G1", "sh1") if g < 16 else ("cG1", "csh1")
                    Gt, St = (G1, sh1) if g < 16 else (cG1, csh1)
                    xt, xk = xts.next()
                    dma("sp", xt, src, [], [xk], "ph_x%d" % xk[1])
                    sq, sqk = sqs.next()
                    act(junk, xt, AF.Square, [xk], ["junk", sqk], accum_out=sq[:, 0:1])
                    rstd_from_ssq(sq[:, 0:1], 1, 1.0 / D, [sqk], [sqk])
                    hn, hnk = hns.next()
                    stt(hn, xt, sq[:, 0:1], Gt, ALU.mult, ALU.mult, [xk, sqk, Gk], [hnk])
                    hb, hbk = hbs.next()
                    tt("pool", hb, hn, St, ALU.add, [hnk, Sk], [hbk])
                    stageB()
                    for (g_, ns_, hT3_, hTk_) in pendStore:
                        dma("sp", hT[:, :, g_ * 512: g_ * 512 + ns_ * 128], hT3_[:, :, 0:ns_ * 128],
                            [(hTk_, s_) for s_ in range(ns_)], [("hT", g_)], "ph_st%d" % hTk_[1])
                    del pendStore[:]
                    pendB.append((hb, hbk, hT3, hTk, s))
                pendStore.append((g, nsub, hT3, hTk))
            stageB()
            for (g_, ns_, hT3_, hTk_) in pendStore:
                dma("sp", hT[:, :, g_ * 512: g_ * 512 + ns_ * 128], hT3_[:, :, 0:ns_ * 128],
                    [(hTk_, s_) for s_ in range(ns_)], [("hT", g_)], "ph_st%d" % hTk_[1])
            P.barrier()

        def load_w(dst3, src, nk, name, dkey):
            keys = []
            for k in range(nk):
                kk = (name, k)
                dma("pool", dst3[:, k, :], src[k * 128:(k + 1) * 128, :], [], [kk], dkey)
                keys.append(kk)
            return keys

        def load_hT(hts, g, ntok, pfx):
            hTt, hk = hts.next()
            h3 = hTt.rearrange("p (k t) -> p k t", t=512)
            dma("sp", h3[:, :, 0:ntok], hT[:, :, g * 512: g * 512 + ntok], [("hT", g)], [hk], "%s_h%d" % (pfx, hk[1]))
            return h3, hk

        def phase_rope(col0, dst, with_ctx, pfx):
            A.reset()
            W = A.b16(8 * 1024).rearrange("p (k n) -> p k n", n=1024)
            wk = load_w(W, w_in[:, col0:col0 + 1024], 8, pfx + "W", pfx + "_w")
            hts = Slots(pfx + "ht", [A.b16(8 * 512) for _ in range(2)])
            coss = Slots(pfx + "cos", [A.f32(512) for _ in range(2)])
            sins = Slots(pfx + "sin", [A.f32(512) for _ in range(2)])
            t1s = Slots(pfx + "t1", [A.f32(512) for _ in range(2)])
            t2s = Slots(pfx + "t2", [A.f32(512) for _ in range(2)])
            qrs = Slots(pfx + "qr", [A.b16(512) for _ in range(3)])
            outs = Slots(pfx + "out", [A.b16(8 * 512) for _ in range(2)])
            ng = 17 if with_ctx else 16

            def loads(g):
                ntok = 512 if g < 16 else 256
                h3, hk = load_hT(hts, g, ntok, pfx)
                ct = ck = sn = sk = None
                if g < 16:
                    ct, ck = coss.next()
                    sn, sk = sins.next()
                    dma("sp", ct, ropecos[:, g * 512:(g + 1) * 512], [], [ck], "%s_c%d" % (pfx, ck[1]))
                    dma("sp", sn, ropesin[:, g * 512:(g + 1) * 512], [], [sk], "%s_s%d" % (pfx, sk[1]))
                return (h3, hk, ct, ck, sn, sk)

            pendR = []

            def runR():
                for f_ in pendR:
                    f_()
                del pendR[:]

            nxt = loads(0)
            for g in range(ng):
                ntok = 512 if g < 16 else 256
                h3, hk, ct, ck, sn, sk = nxt
                nxt = loads(g + 1) if g + 1 < ng else None
                ot, ok = outs.next()
                o3 = ot.rearrange("p (h t) -> p h t", t=512)
                for h in range(8):
                    bA, kA, _ = next_bank()
                    for k in range(8):
                        mm(bA[:, 0:ntok], W[:, k, h * 128:(h + 1) * 128], h3[:, k, 0:ntok], k == 0, k == 7,
                           [hk] + wk, [kA])
                    if g < 16:
                        qr, qrk = qrs.next()
                        cp("act", qr, bA, [kA], [qrk])
                        runR()

                        def fin(bA=bA, kA=kA, qr=qr, qrk=qrk, ct=ct, ck=ck, sn=sn, sk=sk, o3=o3, ok=ok, h=h):
                            bB, kB, _ = next_bank()
                            mm(bB, Psw, qr, True, True, [qrk, "Psw"], [kB])
                            t1, t1k = t1s.next()
                            t2, t2k = t2s.next()
                            tt("dve", t1, bA, ct, ALU.mult, [kA, ck, qrk], [t1k])
                            tt("dve", t2, bB, sn, ALU.mult, [kB, sk], [t2k])
                            tt("pool", o3[:, h, :], t1, t2, ALU.add, [t1k, t2k], [(ok, h)])

                        pendR.append(fin)
                    else:
                        cp("act", o3[:, h, 0:ntok], bA[:, 0:ntok], [kA], [(ok, h)])
                runR()
                dma("sp", dst[:, :, g * 512: g * 512 + ntok].rearrange("h p t -> p h t"), o3[:, :, 0:ntok],
                    [(ok, h) for h in range(8)], [(pfx + "dst", g)], "%s_st%d" % (pfx, ok[1]))
            P.barrier()

        def phaseV():
            A.reset()
            pfx = "pv"
            W = A.b16(8 * 1024).rearrange("p (k n) -> p k n", n=1024)
            wk = load_w(W, w_in[:, 2048:3072], 8, "pvW", "pv_w")
            hts = Slots("pvht", [A.b16(8 * 512) for _ in range(2)])
            vts = Slots("pvvt", [A.b16(1024) for _ in range(3)])
            nxt = load_hT(hts, 0, 512, pfx)
            for g in range(17):
                nsub = 4 if g < 16 else 2
                h3, hk = nxt
                nxt = load_hT(hts, g + 1, 512 if g + 1 < 16 else 256, pfx) if g + 1 < 17 else None
                for s in range(nsub):
                    pr, pk, _ = next_pair()
                    for half in range(2):
                        for k in range(8):
                            mm(pr[:, half * 512:(half + 1) * 512], h3[:, k, s * 128:(s + 1) * 128],
                               W[:, k, half * 512:(half + 1) * 512], k == 0, k == 7, [hk] + wk, [pk[half]])
                    vt, vk = vts.next()
                    cp("act", vt[:, 0:512], pr[:, 0:512], [pk[0]], [(vk, 0)])
                    cp("dve", vt[:, 512:1024], pr[:, 512:1024], [pk[1]], [(vk, 1)])
                    tok0 = g * 512 + s * 128
                    dma("sp", vtok[tok0:tok0 + 128, :], vt, [(vk, 0), (vk, 1)], [("vtok", tok0 // 128)], "pv_st%d" % vk[1])
            P.barrier()

        def phaseG():
            A.reset()
            pfx = "pg"
            Wgo = A.b16(8 * 1024).rearrange("p (k n) -> p k n", n=1024)
            Wmg = A.b16(8 * 2048).rearrange("p (k n) -> p k n", n=2048)
            wgk = load_w(Wgo, w_in[:, 5152:6176], 8, "pgWgo", "pg_w1")
            wmk = load_w(Wmg, w_in[:, 6176:8224], 8, "pgWmg", "pg_w2")
            hts = Slots("pght", [A.b16(8 * 512) for _ in range(2)])
            sgs = Slots("pgsg", [A.f32(1024) for _ in range(2)])
            gos = Slots("pggo", [A.b16(1024) for _ in range(2)])
            mgs = Slots("pgmg", [A.b16(4 * 512) for _ in range(2)])
            nxt = load_hT(hts, 0, 512, pfx)
            for g in range(16):
                h3, hk = nxt
                nxt = load_hT(hts, g + 1, 512, pfx) if g + 1 < 16 else None
                for s in range(4):
                    pr, pk, _ = next_pair()
                    for half in range(2):
                        for k in range(8):
                            mm(pr[:, half * 512:(half + 1) * 512], h3[:, k, s * 128:(s + 1) * 128],
                               Wgo[:, k, half * 512:(half + 1) * 512], k == 0, k == 7, [hk] + wgk, [pk[half]])
                    sg, sgk = sgs.next()
                    act(sg, pr, AF.Sigmoid, pk, [sgk])
                    go, gok = gos.next()
                    tt("dve", go, pr, sg, ALU.mult, pk + [sgk], [gok])
                    tok0 = g * 512 + s * 128
                    dma("sp", gout[tok0:tok0 + 128, :], go, [gok], [("gout", tok0 // 128)], "pg_st%d" % gok[1])
                for f4 in range(4):
                    mg, mgk = mgs.next()
                    m3 = mg.rearrange("p (f t) -> p f t", t=512)
                    for fi in range(4):
                        f = f4 * 4 + fi
                        bA, kA, _ = next_bank()
                        for k in range(8):
                            mm(bA, Wmg[:, k, f * 128:(f + 1) * 128], h3[:, k, :], k == 0, k == 7, [hk] + wmk, [kA])
                        act(m3[:, fi, :], bA, AF.Sigmoid, [kA], [(mgk, fi)])
                    dma("sp", gatesT[f4 * 4:(f4 + 1) * 4, :, g * 512:(g + 1) * 512].rearrange("f p t -> p f t"), m3,
                        [(mgk, fi) for fi in range(4)], [("gatesT", g, f4)], "pg_sm%d" % mgk[1])
            P.barrier()

        def phaseGLAprep():
            A.reset()
            pfx = "pa"
            Wq = A.b16(8 * 512).rearrange("p (k n) -> p k n", n=512)
            Wk = A.b16(8 * 512).rearrange("p (k n) -> p k n", n=512)
            Wv = A.b16(8 * 1024).rearrange("p (k n) -> p k n", n=1024)
            Wl = A.b16(8 * 32).rearrange("p (k n) -> p k n", n=32)
            wqk = load_w(Wq, w_in[:, 3072:3584], 8, "paWq", "pa_w1")
            wkk = load_w(Wk, w_in[:, 3584:4096], 8, "paWk", "pa_w2")
            wvk = load_w(Wv, w_in[:, 4096:5120], 8, "paWv", "pa_w3")
            wlk = load_w(Wl, w_in[:, 5120:5152], 8, "paWl", "pa_w4")
            WGU = A.b16(2 * 512)
            WGU3 = WGU.rearrange("p (z n) -> p z n", n=512)
            dma("pool", WGU3[0:17, :, :], wgu.rearrange("z r n -> r z n"), [], ["WGU"], "pa_wgu")
            tri = {}
            for nm, stp, cm, bs in (("TriF", -1, 1, 0), ("MF", 1, -1, 1), ("TriB", 1, -1, 0), ("MB", -1, 1, 1)):
                tl = A.f32(128)
                P.add("pool", lambda e, tl=tl, stp=stp, cm=cm, bs=bs: e.affine_select(
                    out=tl, in_=zer[:, 0:128], pattern=[[stp, 128]], compare_op=ALU.is_gt, fill=-1.0 / 16, base=bs,
                    channel_multiplier=cm), reads=["zer"], writes=[nm])
                tri[nm] = tl
            hts = Slots("paht", [A.b16(8 * 512) for _ in range(2)])
            gls = [Slots("pagl%d" % z, [A.b16(512) for _ in range(2)]) for z in range(2)]
            for z in range(2):
                for i, a_ in enumerate(gls[z].aps):
                    memset("pool", a_[0:32, :], 1.0, [("pagl%d" % z, i)])
            Ls = [A.f32(4 * 512) for _ in range(2)]
            tmps = Slots("patmp", [A.f32(512) for _ in range(2)])
            eks = Slots("paek", [A.f32(512) for _ in range(2)])
            khs = Slots("pakh", [A.b16(512) for _ in range(3)])
            vts = Slots("pavt", [A.b16(1024) for _ in range(2)])
            ebs = Slots("paeb", [A.f32(512) for _ in range(2)])
            ens = Slots("paen", [A.f32(512) for _ in range(2)])
            qks = {}
            for nm in ("QF", "KF", "QB", "KB"):
                qks[nm] = Slots("pa" + nm, [A.b16(4 * 512) for _ in range(2)])
            dsts = {"QF": QF, "KF": KF, "QB": QB, "KB": KB}
            DEC = [DECF.rearrange("p (c h) -> p c h", h=4), DECB.rearrange("p (c h) -> p c h", h=4)]
            nxt = load_hT(hts, 0, 512, pfx)
            for g in range(17):
                nsub = 4 if g < 16 else 2
                ntok = nsub * 128
                h3, hk = nxt
                nxt = load_hT(hts, g + 1, 512 if g + 1 < 16 else 256, pfx) if g + 1 < 17 else None
                glz = []
                for z in range(2):
                    bA, kA, _ = next_bank()
                    for k in range(8):
                        mm(bA[0:16, 0:ntok], Wl[:, k, z * 16:(z + 1) * 16], h3[:, k, 0:ntok], k == 0, k == 7, [hk] + wlk, [kA])
                    gl, glk = gls[z].next()
                    cp("dve", gl[0:16, 0:ntok], bA[0:16, 0:ntok], [kA], [glk])
                    glz.append((gl, glk))
                L3 = [Ls[z].rearrange("p (s n) -> p s n", n=512) for z in range(2)]
                for s in range(nsub):
                    tok0 = g * 512 + s * 128
                    bK, kK, _ = next_bank()
                    for k in range(8):
                        mm(bK, h3[:, k, s * 128:(s + 1) * 128], Wk[:, k, :], k == 0, k == 7, [hk] + wkk, [kK])
                    for z in range(2):
                        gl, glk = glz[z]
                        bG, kG, _ = next_bank()
                        mm(bG, gl[0:17, s * 128:(s + 1) * 128], WGU3[0:17, z, :], True, True, [glk, "WGU"], [kG])
                        tm, tmk = tmps.next()
                        act(tm, bG, AF.Exp, [kG], [tmk], scale=-1.0)
                        act(L3[z][:, s, :], tm, AF.Ln, [tmk, "c_one"], [("paL", z, s)], bias=c_one[:, 0:1])
                        bE, kE, _ = next_bank()
                        mm(bE, tri["MF" if z == 0 else "MB"], L3[z][:, s, :], True, True,
                           [("paL", z, s), "MF", "MB"], [kE])
                        ek, ekk = eks.next()
                        act(ek, bE, AF.Exp, [kE], [ekk])
                        kh, khk = khs.next()
                        tt("dve", kh, bK, ek, ALU.mult, [kK, ekk], [khk])
                        dma("sp", (KHF if z == 0 else KHB)[tok0:tok0 + 128, :], kh, [khk],
                            [("KH", z, tok0 // 128)], "pa_kh%d" % khk[1])
                    pr, pk, _ = next_pair()
                    for half in range(2):
                        for k in range(8):
                            mm(pr[:, half * 512:(half + 1) * 512], h3[:, k, s * 128:(s + 1) * 128],
                               Wv[:, k, half * 512:(half + 1) * 512], k == 0, k == 7, [hk] + wvk, [pk[half]])
                    vt, vk = vts.next()
                    cp("act", vt[:, 0:512], pr[:, 0:512], [pk[0]], [(vk, 0)])
                    cp("dve", vt[:, 512:1024], pr[:, 512:1024], [pk[1]], [(vk, 1)])
                    dma("sp", GV[tok0:tok0 + 128, :], vt, [(vk, 0), (vk, 1)], [("GV", tok0 // 128)], "pa_gv%d" % vk[1])
                tl = {}
                for nm in ("QF", "KF", "QB", "KB"):
                    t_, k_ = qks[nm].next()
                    tl[nm] = (t_.rearrange("p (h t) -> p h t", t=512), k_)
                for h in range(4):
                    bQ, kQ, _ = next_bank()
                    for k in range(8):
                        mm(bQ[:, 0:ntok], Wq[:, k, h * 128:(h + 1) * 128], h3[:, k, 0:ntok], k == 0, k == 7, [hk] + wqk, [kQ])
                    bKt, kKt, _ = next_bank()
                    for k in range(8):
                        mm(bKt[:, 0:ntok], Wk[:, k, h * 128:(h + 1) * 128], h3[:, k, 0:ntok], k == 0, k == 7, [hk] + wkk, [kKt])
                    for z in range(2):
                        bT, kT_, _ = next_bank()
                        for s in range(nsub):
                            mm(bT[:, s * 128:(s + 1) * 128], L3[z][:, s, h * 128:(h + 1) * 128],
                               tri["TriF" if z == 0 else "TriB"], True, True, [("paL", z, s), "TriF", "TriB"], [kT_])
                        eb, ebk = ebs.next()
                        en, enk = ens.next()
                        act(eb[:, 0:ntok], bT[:, 0:ntok], AF.Exp, [kT_], [ebk])
                        act(en[:, 0:ntok], bT[:, 0:ntok], AF.Exp, [kT_], [enk], scale=-1.0)
                        qn, kn = ("QF", "KF") if z == 0 else ("QB", "KB")
                        stt(tl[qn][0][:, h, 0:ntok], bQ[:, 0:ntok], 128.0 ** -0.5, eb[:, 0:ntok], ALU.mult, ALU.mult,
                            [kQ, ebk], [(tl[qn][1], h)])
                        tt("dve", tl[kn][0][:, h, 0:ntok], bKt[:, 0:ntok], en[:, 0:ntok], ALU.mult, [kKt, enk], [(tl[kn][1], h)])
                        col = 127 if z == 0 else 0
                        ebv = eb.rearrange("p (s t) -> p s t", t=128)
                        cp("pool", DEC[z][:, g * 4:g * 4 + nsub, h], ebv[:, 0:nsub, col], [ebk], [("DEC", z, g, h)])
                for nm in ("QF", "KF", "QB", "KB"):
                    t3, k_ = tl[nm]
                    dma("sp", dsts[nm][:, :, g * 512:g * 512 + ntok].rearrange("h p t -> p h t"), t3[:, :, 0:ntok],
                        [(k_, h) for h in range(4)], [(nm, g)], "pa_%s%d" % (nm, k_[1]))
            P.barrier()

        def phaseAttn():
            A.reset()
            NKT = TT // 128
            Ks = Slots("abK", [A.b16(TT) for _ in range(2)])
            Vs = Slots("abV", [A.b16(NKT * 129) for _ in range(2)])
            for i, v_ in enumerate(Vs.aps):
                memset("pool", v_, 1.0, [("abV", i)])
            Qs = Slots("abQ", [A.b16(512) for _ in range(3)])
            Es = Slots("abE", [A.b16(1024) for _ in range(3)])
            sm = Slots("absm", [A.f32(16) for _ in range(2)])
            tms = Slots("abt", [A.f32(128) for _ in range(2)])
            os_ = Slots("abo", [A.f32(128) for _ in range(2)])
            jk = A.f32(128)
            ybs = Slots("aby", [A.b16(128) for _ in range(8)])
            yTs = Slots("abyT", [A.b16(512) for _ in range(3)])
            spair = [0]
            acc0 = bank(4)
            acc1 = bank(5)
            acc2 = bank(6)

            def accreg(c, qs):
                if qs < 3:
                    b = acc0 if c == 0 else acc1
                    return b[:, qs * 129:(qs + 1) * 129], ("ps", 4 + c)
                return acc2[:, c * 129:(c + 1) * 129], ("ps", 6)

            accsb = Slots("abacc", [A.f32(3 * 512) for _ in range(2)])
            heads = {}

            def load_head(h):
                Kt, Kk = Ks.next()
                Vt, Vk = Vs.next()
                V3 = Vt.rearrange("p (n e) -> p n e", e=129)
                dma("sp", Kt, kT[h, :, :], [("pkdst", g) for g in range(17)], [Kk], "ab_k%d" % Kk[1])
                vsrc = vtok[:, h * 128:(h + 1) * 128].rearrange("(n p) e -> p n e", p=128)
                for j in range(3):
                    dma("sp", V3[:, j * 22:(j + 1) * 22, 0:128], vsrc[:, j * 22:(j + 1) * 22, :],
                        [("vtok", n) for n in range(j * 22, (j + 1) * 22)] + [Vk], [(Vk, "d", j)], "ab_v%d_%d" % (Vk[1], j))
                heads[h] = (Kt, Kk, V3, [Vk, (Vk, "d", 0), (Vk, "d", 1), (Vk, "d", 2)])

            qtiles = {}

            def load_q(h, qb):
                Qt, Qk = Qs.next()
                dma("sp", Qt, qT[h, :, qb * 512:(qb + 1) * 512], [("pqdst", qb)], [Qk], "ab_q%d" % Qk[1])
                qtiles[(h, qb)] = (Qt, Qk)

            iters = [(h, qb, kt) for h in range(8) for qb in range(16) for kt in range(NKT)]
            Sbuf = {}

            def emit_S(i):
                h, qb, kt = iters[i]
                Kt, Kk, V3, Vkeys = heads[h]
                Qt, Qk = qtiles[(h, qb)]
                sp_ = i % 2
                S = ps[:, sp_ * 1024:(sp_ + 1) * 1024]
                Sk = [("ps", 2 * sp_), ("ps", 2 * sp_ + 1)]
                for c in range(2):
                    mm(S[:, c * 512:(c + 1) * 512], Kt[c * 64:(c + 1) * 64, kt * 128:(kt + 1) * 128],
                       Qt[c * 64:(c + 1) * 64, :], True, True, [Kk, Qk], [Sk[c]])
                Sbuf[i] = (S, Sk)

            def finish(h, qb):
                acck = [("ps", 4), ("ps", 5), ("ps", 6)]
                ab, abk = accsb.next()
                for j in range(3):
                    n_ = 387 if j < 2 else 258
                    cp("dve", ab[:, j * 512:j * 512 + n_], bank(4 + j)[:, 0:n_], [acck[j]], [(abk, j)])
                abks = [(abk, j) for j in range(3)]

                def sreg(c, qs):
                    if qs < 3:
                        return ab[:, c * 512 + qs * 129: c * 512 + (qs + 1) * 129]
                    return ab[:, 1024 + c * 129: 1024 + (c + 1) * 129]

                s_, sk_ = sm.next()
                for c in range(2):
                    for qs in range(4):
                        reg = sreg(c, qs)
                        P.add("dve", lambda e, o_=s_[:, c * 4 + qs: c * 4 + qs + 1], i_=reg[:, 128:129]: e.reciprocal(out=o_, in_=i_),
                              reads=abks, writes=[(sk_, c * 4 + qs)])
                ts("dve", s_[:, 4:8], s_[:, 4:8], lamcol[:, 0:1], None, ALU.mult, ALU.bypass,
                   [(sk_, j) for j in range(4, 8)] + ["lamcol"], [(sk_, "w")])
                yT, yTk = yTs.next()
                for qs in range(4):
                    r0 = sreg(0, qs)
                    r1 = sreg(1, qs)
                    tm, tmk = tms.next()
                    ts("dve", tm, r1[:, 0:128], s_[:, 4 + qs:5 + qs], None, ALU.mult, ALU.bypass,
                       abks + [(sk_, "w")], [tmk])
                    o_, ok_ = os_.next()
                    stt(o_, r0[:, 0:128], s_[:, qs:qs + 1], tm, ALU.mult, ALU.subtract,
                        abks + [(sk_, qs), tmk], [ok_])
                    stt(jk, o_, 1.0, o_, ALU.mult, ALU.mult, [ok_], ["abjk", (sk_, 8 + qs)],
                        accum_out=s_[:, 8 + qs:9 + qs])
                    rstd_from_ssq(s_[:, 8 + qs:9 + qs], 1, 1.0 / 128, [(sk_, 8 + qs)], [(sk_, 8 + qs)])
                    yb, ybk = ybs.next()
                    stt(yb, o_, s_[:, 8 + qs:9 + qs], hnda, ALU.mult, ALU.mult, [ok_, (sk_, 8 + qs), "hnda"], [ybk])
                    pend_tr.append((qs, yb, ybk, yT, yTk, h, qb))

            pend_tr = []

            def flush_tr():
                if not pend_tr:
                    return
                b7 = bank(7).bitcast(BF16)
                for (qs, yb, ybk, yT, yTk, h, qb) in pend_tr:
                    P.add("pe", lambda e, o2=b7[:, qs * 128:(qs + 1) * 128], i2=yb: e.transpose(o2, i2, ident),
                          reads=[ybk, "ident"], writes=[("ps", 7)])
                (qs, yb, ybk, yT, yTk, h, qb) = pend_tr[0]
                cp("dve", yT, b7[:, 0:512], [("ps", 7)], [yTk])
                dma("sp", ydaT[h, :, qb * 512:(qb + 1) * 512], yT, [yTk], [("ydaT", h, qb)], "ab_y%d" % yTk[1])
                del pend_tr[:]

            load_head(0)
            load_q(0, 0)
            nI = len(iters)

            def prep_S(j):
                if j >= nI:
                    return
                hn_, qbn_, ktn_ = iters[j]
                if hn_ not in heads:
                    load_head(hn_)
                if (hn_, qbn_) not in qtiles:
                    load_q(hn_, qbn_)
                emit_S(j)

            prep_S(0)
            prep_S(1)
            for i in range(nI):
                h, qb, kt = iters[i]
                if kt == 0:
                    nqb = (h, qb + 1) if qb < 15 else ((h + 1, 0) if h < 7 else None)
                    if nqb is not None and nqb not in qtiles:
                        load_q(*nqb)
                    if qb == 1 and h < 7 and (h + 1) not in heads:
                        load_head(h + 1)
                S, Sk = Sbuf.pop(i)
                Kt, Kk, V3, Vkeys = heads[h]
                E, Ek = Es.next()
                act(E, S, AF.Exp, Sk, [Ek], scale=0.125)
                prep_S(i + 2)
                for c in range(2):
                    for qs in range(4):
                        reg, rk = accreg(c, qs)
                        first = (kt == 0) and ((qs == 0) or (qs == 3 and c == 0))
                        mm(reg, E[:, c * 512 + qs * 128: c * 512 + (qs + 1) * 128], V3[:, kt, :],
                           first, kt == NKT - 1, [Ek] + Vkeys, [rk], skip_group_check=True)
                if kt == 8:
                    flush_tr()
                if kt == NKT - 1:
                    finish(h, qb)
            flush_tr()
            P.barrier()

        def phaseGLA():
            A.reset()
            C = {}
            for z in range(2):
                pfx = "gf" if z == 0 else "gb"
                c = C[z] = {}
                c["pfx"] = pfx
                c["S32"] = A.f32(4 * 256)
                c["S16"] = A.b16(4 * 256)
                memset("pool", c["S32"], 0.0, [(pfx + "S32", h) for h in range(4)])
                memset("pool", c["S16"], 0.0, [(pfx + "S16", h) for h in range(4)])
                c["q"] = Slots(pfx + "q", [A.b16(512) for _ in range(3)])
                c["k"] = Slots(pfx + "k", [A.b16(512) for _ in range(3)])
                c["kh"] = Slots(pfx + "kh", [A.b16(512) for _ in range(3)])
                c["v"] = Slots(pfx + "v", [A.b16(1024) for _ in range(3)])
                c["am"] = Slots(pfx + "am", [A.b16(512) for _ in range(2)])
                c["ob"] = Slots(pfx + "ob", [A.f32(1024) for _ in range(3)])
                c["of"] = Slots(pfx + "of", [A.f32(1024) for _ in range(3)])
                c["go"] = Slots(pfx + "go", [A.b16(1024) for _ in range(3)])
                c["sq"] = Slots(pfx + "sq", [A.f32(4) for _ in range(3)])
                c["yb"] = Slots(pfx + "yb", [A.b16(1024) for _ in range(3)])
                c["yT"] = Slots(pfx + "yT", [A.b16(8 * 128) for _ in range(2)])
                c["jk"] = A.f32(256)
                c["order"] = [64, 65] + list(range(64)) if z == 0 else [65, 64] + list(range(63, -1, -1))
            visited = set()
            gpre = {}

            def gla_loads(z, n):
                c = C[z]
                pfx = c["pfx"]
                Qd, Kd, KHd = (QF, KF, KHF) if z == 0 else (QB, KB, KHB)
                Qn, Kn = ("QF", "KF") if z == 0 else ("QB", "KB")
                t0 = n * 128
                g = n // 4
                kh, khk = c["kh"].next()
                vt, vk = c["v"].next()
                q3 = qk = k3 = kk = None
                if n < 64:
                    qt, qk = c["q"].next()
                    kt_, kk = c["k"].next()
                    q3 = qt.rearrange("p (h t) -> p h t", t=128)
                    k3 = kt_.rearrange("p (h t) -> p h t", t=128)
                    dma("sp", q3, Qd[:, :, t0:t0 + 128].rearrange("h p t -> p h t"), [(Qn, g)], [qk], "%s_q%d" % (pfx, qk[1]))
                    dma("sp", k3, Kd[:, :, t0:t0 + 128].rearrange("h p t -> p h t"), [(Kn, g)], [kk], "%s_k%d" % (pfx, kk[1]))
                dma("sp", kh, KHd[t0:t0 + 128, :], [("KH", z, n)], [khk], "%s_kh%d" % (pfx, khk[1]))
                dma("sp", vt, GV[t0:t0 + 128, :], [("GV", n)], [vk], "%s_v%d" % (pfx, vk[1]))
                gpre[(z, n)] = (kh, khk, vt, vk, q3, qk, k3, kk)

            def step(z, n):
                c = C[z]
                pfx = c["pfx"]
                Qd, Kd, KHd = (QF, KF, KHF) if z == 0 else (QB, KB, KHB)
                Qn, Kn = ("QF", "KF") if z == 0 else ("QB", "KB")
                mask = maskF if z == 0 else maskB
                maskk = "maskF" if z == 0 else "maskB"
                DEC = (DECF if z == 0 else DECB).rearrange("p (c h) -> p c h", h=4)
                S32, S16 = c["S32"], c["S16"]
                t0 = n * 128
                g = n // 4
                is_ctx = n >= 64
                second = n in visited
                visited.add(n)
                if (z, n) not in gpre:
                    gla_loads(z, n)
                (kh, khk, vt, vk, q3, qk, k3, kk) = gpre.pop((z, n))
                if not is_ctx:
                    if second:
                        oft, ofk = c["of"].next()
                        dma("sp", oft, o_f[t0:t0 + 128, :], [("o_f", n)], [ofk], "%s_of%d" % (pfx, ofk[1]))
                        got, gok = c["go"].next()
                        dma("sp", got, gout[t0:t0 + 128, :], [("gout", n)], [gok], "%s_go%d" % (pfx, gok[1]))
                    bA, kA, _ = next_bank()
                    for h in range(4):
                        mm(bA[:, h * 128:(h + 1) * 128], k3[:, h, :], q3[:, h, :], True, True, [kk, qk], [kA])
                    am, amk = c["am"].next()
                    tt("dve", am, bA, mask, ALU.mult, [kA, maskk], [amk])
                    pr = [next_bank(), next_bank()]
                    for h in range(4):
                        ob, obk, _ = pr[h // 2]
                        oreg = ob[:, (h % 2) * 256:(h % 2 + 1) * 256]
                        mm(oreg, q3[:, h, :], S16[:, h * 256:(h + 1) * 256], True, False, [qk, (pfx + "S16", h)], [obk])
                        mm(oreg, am[:, h * 128:(h + 1) * 128], vt[:, h * 256:(h + 1) * 256], False, True, [amk, vk], [obk])
                pu = [next_bank(), next_bank()]
                for h in range(4):
                    ub, ubk, _ = pu[h // 2]
                    ureg = ub[:, (h % 2) * 256:(h % 2 + 1) * 256]
                    mm(ureg, kh[:, h * 128:(h + 1) * 128], vt[:, h * 256:(h + 1) * 256], True, True, [khk, vk], [ubk])
                for h in range(4):
                    ub, ubk, _ = pu[h // 2]
                    ureg = ub[:, (h % 2) * 256:(h % 2 + 1) * 256]
                    stt(S32[:, h * 256:(h + 1) * 256], S32[:, h * 256:(h + 1) * 256], DEC[:, n, h:h + 1], ureg,
                        ALU.mult, ALU.add, [(pfx + "S32", h), ubk] + [("DEC", z, n // 4, h)], [(pfx + "S32", h)])
                    cp("act", S16[:, h * 256:(h + 1) * 256], S32[:, h * 256:(h + 1) * 256], [(pfx + "S32", h)], [(pfx + "S16", h)])
                if is_ctx:
                    return
                ot, otk = c["ob"].next()
                cp("act", ot[:, 0:512], pr[0][0], [pr[0][1]], [(otk, 0)])
                cp("act", ot[:, 512:1024], pr[1][0], [pr[1][1]], [(otk, 1)])
                if not second:
                    dma("sp", o_f[t0:t0 + 128, :], ot, [(otk, 0), (otk, 1)], [("o_f", n)], "%s_st%d" % (pfx, otk[1]))
                    return

                def fin(c=c, pfx=pfx, z=z, n=n, t0=t0, ot=ot, otk=otk, oft=oft, ofk=ofk, got=got, gok=gok):
                    osm, osk = ot, otk
                    tt("dve", osm[:, 0:512], ot[:, 0:512], oft[:, 0:512], ALU.add, [(otk, 0), ofk], [(osk, 0)])
                    tt("pool", osm[:, 512:1024], ot[:, 512:1024], oft[:, 512:1024], ALU.add, [(otk, 1), ofk], [(osk, 1)])
                    sq, sqk = c["sq"].next()
                    jk = c["jk"]
                    for h in range(4):
                        act(jk, osm[:, h * 256:(h + 1) * 256], AF.Square, [(osk, h // 2)], [pfx + "jk", (sqk, h)],
                            accum_out=sq[:, h:h + 1])
                    rstd_from_ssq(sq[:, 0:4], 4, 1.0 / 256, [(sqk, h) for h in range(4)], [(sqk, "r")])
                    yt, ytk = oft, ofk
                    yb, ybk = c["yb"].next()
                    for h in range(4):
                        stt(yt[:, h * 256:(h + 1) * 256], osm[:, h * 256:(h + 1) * 256], sq[:, h:h + 1], hngla, ALU.mult, ALU.mult,
                            [(osk, h // 2), (sqk, "r"), "hngla", ytk], [(ytk, "y", h)])
                    tt("pool", yb, yt, got, ALU.mult, [(ytk, "y", h) for h in range(4)] + [gok], [ybk])
                    pend.append((z, yb, ybk, t0, n))

                pend_fin.append(fin)

            pend_fin = []

            def run_fin():
                for f_ in pend_fin:
                    f_()
                del pend_fin[:]

            pend = []

            def flush():
                for (z, yb, ybk, t0, n) in pend:
                    c = C[z]
                    pfx = c["pfx"]
                    bk, bkk, _ = next_bank()
                    b16v = bk.bitcast(BF16)
                    for k in range(8):
                        P.add("pe", lambda e, k=k, b16v=b16v, yb=yb: e.transpose(b16v[:, k * 128:(k + 1) * 128],
                                                                                 yb[:, k * 128:(k + 1) * 128], ident),
                              reads=[ybk, "ident"], writes=[bkk])
                    yT, yTk = c["yT"].next()
                    cp("act", yT, b16v, [bkk], [yTk])
                    dma("sp", yglaT[:, :, t0:t0 + 128].rearrange("k p t -> p k t"), yT.rearrange("p (k t) -> p k t", t=128),
                        [yTk], [("yglaT", n)], "%s_sy%d" % (pfx, yTk[1]))
                del pend[:]

            gla_loads(0, C[0]["order"][0])
            gla_loads(1, C[1]["order"][0])
            for idx in range(66):
                old = list(pend)
                del pend[:]
                if idx + 1 < 66:
                    gla_loads(0, C[0]["order"][idx + 1])
                    gla_loads(1, C[1]["order"][idx + 1])
                oldf = list(pend_fin)
                del pend_fin[:]
                step(0, C[0]["order"][idx])
                step(1, C[1]["order"][idx])
                newf = list(pend_fin)
                del pend_fin[:]
                pend_fin.extend(oldf)
                run_fin()
                pend_fin.extend(newf)
                new_ = list(pend)
                del pend[:]
                pend.extend(old)
                flush()
                pend.extend(new_)
            run_fin()
            flush()
            flush()
            P.barrier()

        def phaseMerge():
            A.reset()
            pfx = "pm"
            Wda = A.b16(8 * 1024).rearrange("p (k n) -> p k n", n=1024)
            Wgl = A.b16(8 * 1024).rearrange("p (k n) -> p k n", n=1024)
            Wo = A.b16(8 * 1024).rearrange("p (k n) -> p k n", n=1024)
            wdk = load_w(Wda, w_bda, 8, "pmWda", "pm_w1")
            wgk = load_w(Wgl, w_bgla, 8, "pmWgl", "pm_w2")
            wok = load_w(Wo, w_out, 8, "pmWo", "pm_w3")
            PG1 = A.f32(D); sh2 = A.f32(D); G2 = A.f32(D)
            load_rows(PG1, 2, "PG1", "pm_r0")
            load_rows(sh2, 3, "sh2", "pm_r1")
            load_rows(G2, 4, "G2", "pm_r2")
            yds = Slots("pmyd", [A.b16(8 * 512) for _ in range(2)])
            ygs = Slots("pmyg", [A.b16(8 * 512) for _ in range(2)])
            gts = Slots("pmgt", [A.b16(2 * 512) for _ in range(3)])
            t1s = Slots("pmt1", [A.f32(512) for _ in range(2)])
            t2s = Slots("pmt2", [A.f32(512) for _ in range(2)])
            yT = A.b16(8 * 512)
            yT3 = yT.rearrange("p (k t) -> p k t", t=512)
            xts = Slots("pmx", [A.f32(D) for _ in range(3)])
            zs = Slots("pmz", [A.f32(D) for _ in range(3)])
            hns = Slots("pmhn", [A.f32(D) for _ in range(2)])
            hbs = Slots("pmhb", [A.b16(D) for _ in range(4)])
            junk = A.b16(D)
            sqs = Slots("pmsq", [A.f32(4) for _ in range(6)])
            h2s = Slots("pmh2", [A.b16(8 * 512) for _ in range(2)])
            mpend = []

            def flush_m(keep=0):
                n_ = max(len(mpend) - keep, 0)
                for (hb, hbk, h23, h2k, s) in mpend[:n_]:
                    bk, bkk, _ = next_bank()
                    b16v = bk.bitcast(BF16)
                    for k in range(8):
                        P.add("pe", lambda e, k=k, b16v=b16v, hb=hb: e.transpose(b16v[:, k * 128:(k + 1) * 128],
                                                                                 hb[:, k * 128:(k + 1) * 128], ident),
                              reads=[hbk, "ident"], writes=[bkk])
                    cp("act", h23[:, :, s * 128:(s + 1) * 128], b16v.rearrange("p (k t) -> p k t", t=128), [bkk], [(h2k, s)])
                del mpend[:n_]

            yTs = [yT, A.b16(8 * 512)]

            def step1(g):
                yTg = yTs[g % 2].rearrange("p (k t) -> p k t", t=512)
                yd, ydk = yds.next()
                yg, ygk = ygs.next()
                yd3 = yd.rearrange("p (k t) -> p k t", t=512)
                yg3 = yg.rearrange("p (k t) -> p k t", t=512)
                dma("sp", yd3, ydaT[:, :, g * 512:(g + 1) * 512].rearrange("k p t -> p k t"),
                    [("ydaT", h, g) for h in range(8)], [ydk], "pm_yd%d" % ydk[1])
                dma("sp", yg3, yglaT[:, :, g * 512:(g + 1) * 512].rearrange("k p t -> p k t"),
                    [("yglaT", g * 4 + s) for s in range(4)], [ygk], "pm_yg%d" % ygk[1])
                for fo in range(8):
                    gt, gtk = gts.next()
                    gt3 = gt.rearrange("p (b t) -> p b t", t=512)
                    dma("sp", gt3[:, 0, :], gatesT[fo, :, g * 512:(g + 1) * 512], [("gatesT", g, fo // 4)], [(gtk, 0)],
                        "pm_ga%d" % gtk[1])
                    dma("sp", gt3[:, 1, :], gatesT[8 + fo, :, g * 512:(g + 1) * 512], [("gatesT", g, 2 + fo // 4)], [(gtk, 1)],
                        "pm_gb%d" % gtk[1])
                    bA, kA, _ = next_bank()
                    for k in range(8):
                        mm(bA, Wda[:, k, fo * 128:(fo + 1) * 128], yd3[:, k, :], k == 0, k == 7, [ydk] + wdk, [kA])
                    bB, kB, _ = next_bank()
                    for k in range(8):
                        mm(bB, Wgl[:, k, fo * 128:(fo + 1) * 128], yg3[:, k, :], k == 0, k == 7, [ygk] + wgk, [kB])
                    t1, t1k = t1s.next()
                    t2, t2k = t2s.next()
                    tt("dve", t1, bA, gt3[:, 0, :], ALU.mult, [kA, (gtk, 0)], [t1k])
                    tt("dve", t2, bB, gt3[:, 1, :], ALU.mult, [kB, (gtk, 1)], [t2k])
                    tt("pool", yTg[:, fo, :], t1, t2, ALU.add, [t1k, t2k], [("pmyT", g % 2, fo)])

            def step2(g):
                yTg = yTs[g % 2].rearrange("p (k t) -> p k t", t=512)
                h2t, h2k = h2s.next()
                h23 = h2t.rearrange("p (k t) -> p k t", t=512)
                for s in range(4):
                    tok0 = g * 512 + s * 128
                    if (g, s) not in xpre:
                        ldx(g, s)
                    xt, xk = xpre.pop((g, s))
                    ns_ = (g, s + 1) if s < 3 else (g + 1, 0)
                    if ns_[0] < 16:
                        ldx(*ns_)
                    pr, pk, _ = next_pair()
                    for half in range(2):
                        for k in range(8):
                            mm(pr[:, half * 512:(half + 1) * 512], yTg[:, k, s * 128:(s + 1) * 128],
                               Wo[:, k, half * 512:(half + 1) * 512], k == 0, k == 7, [("pmyT", g % 2, k)] + wok, [pk[half]])
                    flush_m(1)
                    sq, sqk = sqs.next()
                    act(junk, pr, AF.Square, pk, ["pmjunk", (sqk, 0)], accum_out=sq[:, 0:1])
                    rstd_from_ssq(sq[:, 0:1], 1, 1.0 / D, [(sqk, 0)], [(sqk, 0)])
                    zt, zk = zs.next()
                    stt(zt, pr, sq[:, 0:1], PG1, ALU.mult, ALU.mult, pk + [(sqk, 0), "PG1"], [(zk, "a")])
                    xn, xnk = zt, (zk, "b")
                    tt("pool", xn, zt, xt, ALU.add, [(zk, "a"), xk], [xnk])
                    dma("sp", xnew[tok0:tok0 + 128, :], xn, [xnk], [("xnew", tok0 // 128)], "pm_sx%d" % zk[1])
                    runB()

                    def stageB(xn=xn, xnk=xnk, sq=sq, sqk=sqk, h23=h23, h2k=h2k, s=s):
                        act(junk2, xn, AF.Square, [xnk], ["pmjunk", (sqk, 1)], accum_out=sq[:, 1:2])
                        rstd_from_ssq(sq[:, 1:2], 1, 1.0 / D, [(sqk, 1)], [(sqk, 1)])
                        hn, hnk = hns.next()
                        stt(hn, xn, sq[:, 1:2], G2, ALU.mult, ALU.mult, [xnk, (sqk, 1), "G2"], [hnk])
                        hb, hbk = hbs.next()
                        tt("dve", hb, hn, sh2, ALU.add, [hnk, "sh2"], [hbk])
                        mpend.append((hb, hbk, h23, h2k, s))

                    pendB.append(stageB)
                pend_store.append((g, h23, h2k))

            pend_store = []
            xpre = {}
            pendB = []
            junk2 = junk

            def runB():
                for f_ in pendB:
                    f_()
                del pendB[:]

            def ldx(g, s):
                tok0 = g * 512 + s * 128
                xt, xk = xts.next()
                dma("sp", xt, x[tok0:tok0 + 128, :], [], [xk], "pm_x%d" % xk[1])
                xpre[(g, s)] = (xt, xk)

            def flush_store():
                for (g, h23, h2k) in pend_store:
                    dma("sp", h2T[:, :, g * 512:(g + 1) * 512], h23, [(h2k, s) for s in range(4)], [("h2T", g)], "pm_sh%d" % h2k[1])
                del pend_store[:]

            step1(0)
            for g in range(16):
                runB()
                if g + 1 < 16:
                    step1(g + 1)
                flush_m()
                flush_store()
                step2(g)
            runB()
            flush_m()
            flush_store()
            P.barrier()

        def phaseFFN():
            A.reset()
            W1 = A.b16(8 * DFF).rearrange("p (k n) -> p k n", n=DFF)
            W2 = A.b16(32 * D).rearrange("p (k n) -> p k n", n=D)
            w1k = load_w(W1, w_ff1, 8, "pfW1", "pf_w1")
            w2k = load_w(W2, w_ff2, 32, "pfW2", "pf_w2")
            PG2 = A.f32(D)
            load_rows(PG2, 5, "PG2", "pf_r0")
            NT = 256
            hts = Slots("pfht", [A.b16(8 * NT) for _ in range(2)])
            uT = A.b16(32 * NT)
            uT3 = uT.rearrange("p (f t) -> p f t", t=NT)
            rls = Slots("pfrl", [A.b16(NT) for _ in range(3)])
            xns = Slots("pfxn", [A.f32(D) for _ in range(3)])
            zs = Slots("pfz", [A.f32(D) for _ in range(2)])
            junk = A.b16(D)
            sqs = Slots("pfsq", [A.f32(2) for _ in range(4)])
            def ldh(g):
                ht, hk = hts.next()
                h3 = ht.rearrange("p (k t) -> p k t", t=NT)
                dma("sp", h3, h2T[:, :, g * NT:(g + 1) * NT], [("h2T", (g * NT) // 512)], [hk], "pf_h%d" % hk[1])
                return h3, hk

            xpre = {}

            def ldx(tok0):
                xn, xnk = xns.next()
                dma("sp", xn, xnew[tok0:tok0 + 128, :], [("xnew", tok0 // 128)], [xnk], "pf_x%d" % xnk[1])
                xpre[tok0] = (xn, xnk)

            nxt = ldh(0)
            for g in range(T // NT):
                h3, hk = nxt
                nxt = ldh(g + 1) if g + 1 < T // NT else None
                for fc in range(32):
                    bA, kA, _ = next_bank()
                    for k in range(8):
                        mm(bA[:, 0:NT], W1[:, k, fc * 128:(fc + 1) * 128], h3[:, k, :], k == 0, k == 7, [hk] + w1k, [kA])
                    rl, rlk = rls.next()
                    act(rl, bA[:, 0:NT], AF.Relu, [kA], [rlk])
                    tt("dve" if fc % 2 == 0 else "pool", uT3[:, fc, :], rl, rl, ALU.mult, [rlk], [("pfuT", fc)])
                for s in range(NT // 128):
                    tok0 = g * NT + s * 128
                    if tok0 not in xpre:
                        ldx(tok0)
                    xn, xnk = xpre.pop(tok0)
                    if tok0 + 128 < T:
                        ldx(tok0 + 128)
                    pr, pk, _ = next_pair()
                    for half in range(2):
                        for fc in range(32):
                            mm(pr[:, half * 512:(half + 1) * 512], uT3[:, fc, s * 128:(s + 1) * 128],
                               W2[:, fc, half * 512:(half + 1) * 512], fc == 0, fc == 31, [("pfuT", fc)] + w2k, [pk[half]])
                    sq, sqk = sqs.next()
                    act(junk, pr, AF.Square, pk, ["pfjunk", sqk], accum_out=sq[:, 0:1])
                    rstd_from_ssq(sq[:, 0:1], 1, 1.0 / D, [sqk], [sqk])
                    zt, zk = zs.next()
                    stt(zt, pr, sq[:, 0:1], PG2, ALU.mult, ALU.mult, pk + [sqk, "PG2"], [(zk, "a")])
                    tt("pool", zt, zt, xn, ALU.add, [(zk, "a"), xnk], [(zk, "b")])
                    dma("sp", out[tok0:tok0 + 128, :], zt, [(zk, "b")], [("out", tok0 // 128)], "pf_so%d" % zk[1])
            P.barrier()

        phase0()
        phaseH()
        phase_rope(0, qT, False, "pq")
        phase_rope(1024, kT, True, "pk")
        phaseV()
        phaseG()
        phaseGLAprep()
        phaseAttn()
        phaseGLA()
        phaseMerge()
        phaseFFN()
        P.add("sp", lambda e: e.nop())
        P.finalize(block, lambda name: st.enter_context(nc.semaphore(name)))
        build_program.stats = (len(P.ops), P.n_dma_sems)
    return nc


DEBUG_OUT = set()


def _rope_tables():
    t = np.arange(T)
    inv = (np.float32(10000.0) ** (-(np.arange(0, 32, 2, dtype=np.float32)) / np.float32(32))).astype(np.float32)
    ang_r = (t // 64).astype(np.float32)[:, None] * inv[None, :]
    ang_c = (t % 64).astype(np.float32)[:, None] * inv[None, :]
    cos = np.zeros((128, T), np.float32)
    sin = np.zeros((128, T), np.float32)
    for c in range(2):
        for j in range(64):
            ang = ang_r if j < 32 else ang_c
            f = j % 16
            first = (j % 32) < 16
            cos[c * 64 + j] = np.cos(ang[:, f])
            sin[c * 64 + j] = (-np.sin(ang[:, f])) if first else np.sin(ang[:, f])
    return cos, sin


def _swap_perm():
    perm = np.zeros(1024, np.int64)
    for hc in range(16):
        for j in range(64):
            pj = j + 16 if (j % 32) < 16 else j - 16
            perm[hc * 64 + j] = hc * 64 + pj
    return perm


def make_in_maps(inputs):
    f = lambda a: np.ascontiguousarray(np.asarray(a, dtype=np.float32))
    x = f(inputs["x"]); c = f(inputs["c"]); ctx = f(inputs["ctx"]); c_ctx = f(inputs["c_ctx"])
    w_in = f(inputs["w_in"][0])
    wgu = np.ascontiguousarray(np.concatenate([f(inputs["w_gate_up"][0]), f(inputs["b_gate_up"][0])[:, None, :]], axis=1))
    lambdas = np.ascontiguousarray(np.concatenate([f(inputs["lambda_q1"][0]), f(inputs["lambda_k1"][0]),
                                                   f(inputs["lambda_q2"][0]), f(inputs["lambda_k2"][0])])[None, :])
    norms = np.ascontiguousarray(np.stack([f(inputs["pre_norm1"][0]), f(inputs["post_norm1"][0]),
                                           f(inputs["pre_norm2"][0]), f(inputs["post_norm2"][0])]))
    cos, sin = _rope_tables()
    shared = {
        "w_mod": f(inputs["w_mod"][0]), "b_mod": f(inputs["b_mod"][0])[None, :], "norms": norms,
        "w_in": w_in, "wgu": wgu, "lambdas": lambdas,
        "hn_da": f(inputs["da_head_norm"][0])[None, :], "hn_gla": f(inputs["gla_head_norm"][0])[None, :],
        "w_bda": f(inputs["w_branch_da"][0]), "w_bgla": f(inputs["w_branch_gla"][0]), "w_out": f(inputs["w_out"][0]),
        "w_ff1": f(inputs["w_ff1"][0]), "w_ff2": f(inputs["w_ff2"][0]), "ropecos": cos, "ropesin": sin,
    }
    maps = []
    for b in range(NCORES):
        m = dict(shared)
        m["x"] = x[b]
        m["ctx"] = ctx[b]
        m["cvec"] = np.ascontiguousarray(np.concatenate([c[b].reshape(8, 128), c_ctx.reshape(8, 128)], axis=0))
        maps.append(m)
    return maps


def kernel(**inputs):
    nc = build_program()
    maps = make_in_maps(inputs)
    res = run_bass_kernel_spmd(nc, maps, core_ids=list(range(NCORES)))
    return np.stack([np.asarray(r["out"], dtype=np.float32) for r in res.results], axis=0)
```

```python
import math
from contextlib import ExitStack
import numpy as np
import ml_dtypes
import concourse.bass as bass
import concourse.mybir as mybir
from concourse.bass_utils import run_bass_kernel_spmd

F32 = mybir.dt.float32
BF16 = mybir.dt.bfloat16
AF = mybir.ActivationFunctionType
ALU = mybir.AluOpType

T = 8192
CT = 256
TT = T + CT
D = 1024
DFF = 4096
EPS = 1e-6
LAM_INIT = 0.2
NCORES = 8
ARENA_WORDS = 49152

ENGS = ("pe", "act", "dve", "pool", "sp")
SAME_ENGINE_SYNC = True


class _Op:
    __slots__ = ("eng", "fn", "reads", "writes", "dma_key", "deps", "signal",
                 "sem", "ticket", "idx", "is_dma", "waits")


class Prog:
    def __init__(self, nc):
        self.nc = nc
        self.ops = []
        self.last_writer = {}
        self.readers = {}
        self.phase_last = {}
        self.phase_dmas = []
        self.pending_bar = {}
        self.phase_idx = 0
        self.op_phase = []

    def add(self, eng, fn, reads=(), writes=(), dma_key=None):
        op = _Op()
        op.eng = eng
        op.fn = fn
        op.reads = tuple(reads)
        op.writes = tuple(writes)
        op.dma_key = dma_key
        op.is_dma = dma_key is not None
        op.idx = len(self.ops)
        op.signal = op.is_dma
        deps = set()
        lw = self.last_writer
        rd = self.readers
        for k in op.reads:
            w = lw.get(k)
            if w is not None:
                deps.add(w)
            if type(k) is tuple and k[0] == "ps":
                r = rd.get(k)
                if r:
                    for e_, i_ in r[0].items():
                        if e_ != eng:
                            deps.add(i_)
        for k in op.writes:
            w = lw.get(k)
            if w is not None:
                deps.add(w)
            r = rd.get(k)
            if r:
                deps.update(r[0].values())
                deps.update(r[1])
        if eng in self.pending_bar:
            deps.update(self.pending_bar.pop(eng))
        deps.discard(op.idx)
        op.deps = deps
        for k in op.reads:
            r = rd.get(k)
            if r is None:
                r = rd[k] = ({}, [])
            if op.is_dma:
                r[1].append(op.idx)
            else:
                r[0][eng] = op.idx
        for k in op.writes:
            lw[k] = op.idx
            rd[k] = ({}, [])
        self.ops.append(op)
        self.op_phase.append(self.phase_idx)
        if op.is_dma:
            self.phase_dmas.append(op.idx)
        else:
            self.phase_last[eng] = op.idx
        return op

    def barrier(self):
        deps = set(self.phase_last.values()) | set(self.phase_dmas)
        for e in ENGS:
            s = self.pending_bar.get(e)
            if s is None:
                self.pending_bar[e] = set(deps)
            else:
                s.update(deps)
        self.phase_dmas = []
        self.phase_idx += 1

    def finalize(self, block, new_sem):
        ops = self.ops
        for op in ops:
            keep = set()
            for d in op.deps:
                p = ops[d]
                if (not p.is_dma) and (not op.is_dma) and p.eng == op.eng:
                    if p.eng == "pe" or not SAME_ENGINE_SYNC:
                        continue
                keep.add(d)
                p.signal = True
            op.deps = keep
        eng_sem = {e: new_sem("s_" + e) for e in ENGS}
        dma_sem = {False: {}, True: {}}
        sem_lists = {False: [], True: []}
        cur_phase = -1
        counts = {}
        for op in ops:
            if not op.signal:
                continue
            if op.is_dma:
                if self.op_phase[op.idx] != cur_phase:
                    cur_phase = self.op_phase[op.idx]
                    dma_sem = {False: {}, True: {}}
                sw = op.eng == "pool"
                dsm = dma_sem[sw]
                sem_list = sem_lists[sw]
                s = dsm.get(op.dma_key)
                if s is None:
                    if len(dsm) >= len(sem_list):
                        sem_list.append(new_sem("d%s_%d" % ("s" if sw else "h", len(sem_list))))
                    s = dsm[op.dma_key] = sem_list[len(dsm)]
                op.sem = s
                counts[id(s)] = counts.get(id(s), 0) + 16
                op.ticket = counts[id(s)]
            else:
                s = eng_sem[op.eng]
                op.sem = s
                counts[id(s)] = counts.get(id(s), 0) + 1
                op.ticket = counts[id(s)]
        self.n_dma_sems = len(sem_lists[False]) + len(sem_lists[True])
        waited = {e: {} for e in ENGS}
        for op in ops:
            need = {}
            for d in op.deps:
                p = ops[d]
                key = id(p.sem)
                cur = need.get(key)
                if cur is None or cur[1] < p.ticket:
                    need[key] = (p.sem, p.ticket)
            w = waited[op.eng]
            lst = []
            for key, (s, v) in need.items():
                if w.get(key, 0) >= v:
                    continue
                w[key] = v
                lst.append((s, v))
            op.waits = lst
        per_eng = {e: [] for e in ENGS}
        for op in ops:
            per_eng[op.eng].append(op)

        def emit(e, engobj):
            for op in per_eng[e]:
                for (s, v) in op.waits:
                    engobj.wait_ge(s, v)
                inst = op.fn(engobj)
                if op.signal:
                    inst.then_inc(op.sem, 16 if op.is_dma else 1)

        block.sync(lambda e: emit("sp", e))
        block.tensor(lambda e: emit("pe", e))
        block.scalar(lambda e: emit("act", e))
        block.vector(lambda e: emit("dve", e))
        block.gpsimd(lambda e: emit("pool", e))


class Arena:
    def __init__(self, ap, base, limit):
        self.ap = ap
        self.base = base
        self.off = base
        self.limit = limit

    def reset(self):
        self.off = self.base

    def f32(self, n):
        n = (n + 1) // 2 * 2
        a = self.ap[:, self.off:self.off + n]
        self.off += n
        assert self.off <= self.limit, ("arena overflow", self.off, self.limit)
        return a

    def b16(self, n):
        w = (n + 3) // 4 * 2
        a = self.ap[:, self.off:self.off + w].bitcast(BF16)
        self.off += w
        assert self.off <= self.limit, ("arena overflow", self.off, self.limit)
        return a[:, 0:n]


class Slots:
    def __init__(self, name, aps):
        self.name = name
        self.aps = aps
        self.i = 0

    def next(self):
        k = self.i % len(self.aps)
        self.i += 1
        return self.aps[k], (self.name, k)


_UID = [0]


def uid(p):
    _UID[0] += 1
    return "%s%d" % (p, _UID[0])


def build_program(debug=False):
    nc = bass.Bass("TRN2", target_bir_lowering=False)

    def din(name, shape, dt=F32):
        return nc.dram_tensor(name, shape, dt, kind="ExternalInput").ap()

    def dscr(name, shape, dt):
        kind = "ExternalOutput" if (debug and name in DEBUG_OUT) else "Internal"
        return nc.dram_tensor(name, shape, dt, kind=kind).ap()

    x = din("x", [T, D])
    ctx = din("ctx", [CT, D])
    cvec = din("cvec", [16, 128])
    w_mod = din("w_mod", [D, 6 * D])
    b_mod = din("b_mod", [1, 6 * D])
    norms = din("norms", [4, D])
    w_in = din("w_in", [D, 8224])
    wgu = din("wgu", [2, 17, 512])
    lambdas = din("lambdas", [1, 256])
    hn_da = din("hn_da", [1, 128])
    hn_gla = din("hn_gla", [1, 256])
    w_bda = din("w_bda", [D, D])
    w_bgla = din("w_bgla", [D, D])
    w_out = din("w_out", [D, D])
    w_ff1 = din("w_ff1", [D, DFF])
    w_ff2 = din("w_ff2", [DFF, D])
    ropecos = din("ropecos", [128, T])
    ropesin = din("ropesin", [128, T])
    out = nc.dram_tensor("out", [T, D], F32, kind="ExternalOutput").ap()

    modv = dscr("modv", [8, D], F32)
    hT = dscr("hT", [128, 8, TT], BF16)
    qT = dscr("qT", [8, 128, T], BF16)
    kT = dscr("kT", [8, 128, TT], BF16)
    vtok = dscr("vtok", [TT, D], BF16)
    QF = dscr("QF", [4, 128, TT], BF16)
    KF = dscr("KF", [4, 128, TT], BF16)
    QB = dscr("QB", [4, 128, TT], BF16)
    KB = dscr("KB", [4, 128, TT], BF16)
    KHF = dscr("KHF", [TT, 512], BF16)
    KHB = dscr("KHB", [TT, 512], BF16)
    GV = dscr("GV", [TT, D], BF16)
    gout = dscr("gout", [T, D], BF16)
    gatesT = dscr("gatesT", [16, 128, T], BF16)
    ydaT = dscr("ydaT", [8, 128, T], BF16)
    o_f = dscr("o_f", [T, D], F32)
    yglaT = dscr("yglaT", [8, 128, T], BF16)
    xnew = dscr("xnew", [T, D], F32)
    h2T = dscr("h2T", [128, 8, T], BF16)

    st = ExitStack()
    with st:
        ar = st.enter_context(nc.sbuf_tensor("arena", [128, ARENA_WORDS], F32))
        ps = st.enter_context(nc.psum_tensor("ps", [128, 4096], F32))
        block = st.enter_context(nc.Block())
        P = Prog(nc)

        def bank(i):
            return ps[:, i * 512:(i + 1) * 512]

        psrr = [0]

        def next_bank():
            i = psrr[0] % 8
            psrr[0] += 1
            return bank(i), ("ps", i), i

        def next_pair():
            i = ((psrr[0] + 1) // 2) % 4
            psrr[0] = ((psrr[0] + 1) // 2) * 2 + 2
            return ps[:, i * 1024:(i + 1) * 1024], [("ps", 2 * i), ("ps", 2 * i + 1)], i

        def dma(eng, out_, in_, reads, writes, key):
            P.add(eng, lambda e: e.dma_start(out=out_, in_=in_), reads=reads, writes=writes, dma_key=key)

        def mm(out_, lhsT, rhs, start, stop, reads, writes, **kw):
            P.add("pe", lambda e: e.matmul(out_, lhsT, rhs, start=start, stop=stop, **kw), reads=reads, writes=writes)

        def act(out_, in_, func, reads, writes, **kw):
            P.add("act", lambda e: e.activation(out=out_, in_=in_, func=func, **kw), reads=reads, writes=writes)

        def tt(eng, out_, in0, in1, op, reads, writes):
            P.add(eng, lambda e: e.tensor_tensor(out=out_, in0=in0, in1=in1, op=op), reads=reads, writes=writes)

        def ts(eng, out_, in0, s1, s2, op0, op1, reads, writes):
            P.add(eng, lambda e: e.tensor_scalar(out=out_, in0=in0, scalar1=s1, scalar2=s2, op0=op0, op1=op1),
                  reads=reads, writes=writes)

        def stt(out_, in0, scalar, in1, op0, op1, reads, writes, **kw):
            P.add("dve", lambda e: e.scalar_tensor_tensor(out=out_, in0=in0, scalar=scalar, in1=in1, op0=op0, op1=op1, **kw),
                  reads=reads, writes=writes)

        def cp(eng, out_, in_, reads, writes):
            if eng == "act":
                P.add("act", lambda e: e.activation(out=out_, in_=in_, func=AF.Copy), reads=reads, writes=writes)
            else:
                P.add(eng, lambda e: e.tensor_copy(out=out_, in_=in_), reads=reads, writes=writes)

        def memset(eng, ap, val, writes):
            P.add(eng, lambda e: e.memset(ap, val), writes=writes)

        A = Arena(ar, 0, ARENA_WORDS)
        identf = A.f32(128)
        ident = A.b16(128)
        ones_row = A.f32(128)
        lamcol = A.f32(2)
        hnda = A.f32(128)
        hngla = A.f32(256)
        maskF = A.b16(512)
        maskB = A.b16(512)
        c_eps = A.f32(2)
        c_nhalf = A.f32(4)
        c_one = A.f32(2)
        DECF = A.f32(66 * 4)
        DECB = A.f32(66 * 4)
        zer = A.f32(512)
        A.base = A.off

        memset("pool", zer, 0.0, ["zer"])
        P.add("pool", lambda e: e.affine_select(out=identf, in_=zer[:, 0:128], pattern=[[-1, 128]], compare_op=ALU.not_equal,
                                                fill=1.0, base=0, channel_multiplier=1), reads=["zer"], writes=["identf"])
        cp("dve", ident, identf, ["identf"], ["ident"])
        memset("pool", ones_row, 1.0, ["ones_row"])
        memset("pool", c_eps, EPS, ["c_eps"])
        memset("pool", c_nhalf, -0.5, ["c_nhalf"])
        memset("pool", c_one, 1.0, ["c_one"])
        zer3 = zer.rearrange("p (h t) -> p h t", t=128)
        maskF3 = maskF.rearrange("p (h t) -> p h t", t=128)
        maskB3 = maskB.rearrange("p (h t) -> p h t", t=128)
        P.add("pool", lambda e: e.affine_select(out=maskF3, in_=zer3, pattern=[[0, 4], [-1, 128]], compare_op=ALU.is_gt,
                                                fill=1.0, base=0, channel_multiplier=1), reads=["zer"], writes=["maskF"])
        P.add("pool", lambda e: e.affine_select(out=maskB3, in_=zer3, pattern=[[0, 4], [1, 128]], compare_op=ALU.is_gt,
                                                fill=1.0, base=0, channel_multiplier=-1), reads=["zer"], writes=["maskB"])
        Psw = A.b16(128)
        pswf = A.f32(128)
        memset("pool", pswf, 0.0, ["pswf"])
        pswf3 = pswf.rearrange("p (b j) -> p b j", j=32)
        zer16 = zer[:, 0:64].rearrange("p (b j) -> p b j", j=16)
        P.add("pool", lambda e: e.affine_select(out=pswf3[:, :, 16:32], in_=zer16, pattern=[[32, 4], [1, 16]], compare_op=ALU.not_equal,
                                                fill=1.0, base=0, channel_multiplier=-1), reads=["zer", "pswf"], writes=["pswf"])
        P.add("pool", lambda e: e.affine_select(out=pswf3[:, :, 0:16], in_=zer16, pattern=[[32, 4], [1, 16]], compare_op=ALU.not_equal,
                                                fill=1.0, base=16, channel_multiplier=-1), reads=["zer", "pswf"], writes=["pswf"])
        cp("dve", Psw, pswf, ["pswf"], ["Psw"])
        A.base = A.off
        CONST_KEYS = ["identf", "ident", "ones_row", "c_eps", "c_nhalf", "c_one", "maskF", "maskB"]

        def rstd_from_ssq(ssq, n, inv_n, rk, wk):
            ts("pool", ssq, ssq, inv_n, EPS, ALU.mult, ALU.add, rk, wk)
            tt("pool", ssq, ssq, c_nhalf[:, 0:n], ALU.pow, list(wk) + ["c_nhalf"], wk)

        def phase0():
            A.reset()
            cv = A.f32(128)
            scT = A.f32(16)
            sc2 = A.f32(16)
            modrow = A.f32(6 * D)
            bm = A.f32(6 * D)
            nrm = A.f32(4 * D)
            res = modrow
            wm = [A.f32(8 * 512) for _ in range(2)]
            lamt = A.f32(256)
            lamp = A.f32(128)
            lams = A.f32(4)
            ones2 = ones_row[0:1, 0:2]
            dma("sp", cv[0:16, :], cvec, [], ["cv"], "p0_cv")
            dma("sp", bm[0:1, :], b_mod, [], ["bm"], "p0_bm")
            dma("sp", nrm[0:2, :].rearrange("p (a d) -> p a d", d=D), norms.partition_broadcast(2), [], ["nrm"], "p0_nrm")
            memset("pool", lams, 0.0, ["lams"])
            dma("sp", lamt[0:1, :], lambdas, [], ["lamt"], "p0_lam")
            dma("sp", hnda, hn_da[0, :].partition_broadcast(128), [], ["hnda_raw"], "p0_hnda")
            dma("sp", hngla, hn_gla[0, :].partition_broadcast(128), [], ["hngla"], "p0_hngla")
            ts("dve", hnda, hnda, 1.0 - LAM_INIT, None, ALU.mult, ALU.bypass, ["hnda_raw"], ["hnda"])
            act(cv[0:16, :], cv[0:16, :], AF.Silu, ["cv"], ["cv"])
            b0, k0, _ = next_bank()
            P.add("pe", lambda e: e.transpose(b0[:, 0:16], cv[0:16, :], identf[0:16, 0:16]), reads=["cv", "identf"], writes=[k0])
            cp("dve", scT, b0[:, 0:16], [k0], ["scT"])
            sc2v = sc2.rearrange("p (k v) -> p k v", v=2)
            cp("dve", sc2v[:, :, 0], scT[:, 0:8], ["scT"], ["sc2a"])
            cp("dve", sc2v[:, :, 1], scT[:, 8:16], ["scT"], ["sc2b"])
            for blk in range(12):
                wt = wm[blk % 2]
                wk = ("wm", blk % 2)
                dma("sp", wt.rearrange("p (k n) -> p k n", n=512),
                    w_mod[:, blk * 512:(blk + 1) * 512].rearrange("(k p) n -> p k n", p=128), [], [wk], "p0_wm%d" % (blk % 2))
                b1, k1, _ = next_bank()
                for k in range(8):
                    mm(b1[0:2, :], sc2v[:, k, :], wt[:, k * 512:(k + 1) * 512], k == 0, False,
                       ["sc2a", "sc2b", wk], [k1])
                mm(b1[0:2, :], ones2, bm[0:1, blk * 512:(blk + 1) * 512], False, True, ["ones_row", "bm"], [k1])
                cp("dve", modrow[0:2, blk * 512:(blk + 1) * 512], b1[0:2, :], [k1], [("modrow", blk // 2)])
            mr = lambda j: modrow[0:2, j * D:(j + 1) * D]
            nr = lambda j: nrm[0:2, j * D:(j + 1) * D]
            rs = lambda j: res[0:2, j * D:(j + 1) * D]
            mk = lambda j: ("modrow", j)
            stt(rs(1), mr(1), 1.0, nr(0), ALU.add, ALU.mult, [mk(1), "nrm"], [mk(1)])
            tt("dve", rs(2), mr(2), nr(1), ALU.mult, [mk(2), "nrm"], [mk(2)])
            stt(rs(4), mr(4), 1.0, nr(2), ALU.add, ALU.mult, [mk(4), "nrm"], [mk(4)])
            tt("dve", rs(5), mr(5), nr(3), ALU.mult, [mk(5), "nrm"], [mk(5)])
            rk = [mk(j) for j in range(6)]
            dma("sp", modv[0:6, :], res[0:1, :].rearrange("p (a d) -> p a d", d=D), rk, ["modv"], "p0_mv0")
            dma("sp", modv[6:8, :], res[1:2, 0:2 * D].rearrange("p (a d) -> p a d", d=D), rk, ["modv2"], "p0_mv1")
            l4 = lamt[0:1, :].rearrange("p (a b d) -> p a b d", b=2, d=64)
            lp = lamp[0:1, :].rearrange("p (a d) -> p a d", d=64)
            tt("dve", lp, l4[:, :, 0, :], l4[:, :, 1, :], ALU.mult, ["lamt"], ["lamp"])
            P.add("dve", lambda e: e.tensor_reduce(out=lams[0:1, 0:2], in_=lp, op=ALU.add, axis=mybir.AxisListType.X),
                  reads=["lamp", "lams"], writes=["lams"])
            act(lams[0:1, 0:2], lams[0:1, 0:2], AF.Exp, ["lams"], ["lams"])
            tt("dve", lams[0:1, 2:3], lams[0:1, 0:1], lams[0:1, 1:2], ALU.subtract, ["lams"], ["lams2"])
            ts("dve", lams[0:1, 2:3], lams[0:1, 2:3], LAM_INIT, None, ALU.add, ALU.bypass, ["lams2"], ["lams2"])
            b2, k2, _ = next_bank()
            mm(b2[:, 0:2], ones_row[0:1, :], lams[0:1, 2:4], True, True, ["ones_row", "lams2"], [k2])
            cp("dve", lamcol[:, 0:1], b2[:, 0:1], [k2], ["lamcol"])

        def load_rows(dst, rows, key, dkey):
            dma("sp", dst, modv[rows, :].partition_broadcast(128), ["modv", "modv2"], [key], dkey)

        def phaseH():
            G1 = A.f32(D); sh1 = A.f32(D); cG1 = A.f32(D); csh1 = A.f32(D)
            load_rows(sh1, 0, "sh1", "ph_r0")
            load_rows(G1, 1, "G1", "ph_r1")
            load_rows(csh1, 6, "csh1", "ph_r2")
            load_rows(cG1, 7, "cG1", "ph_r3")
            xts = Slots("xt", [A.f32(D) for _ in range(4)])
            junk = A.b16(D)
            hns = Slots("hn", [A.f32(D) for _ in range(2)])
            hbs = Slots("hb", [A.b16(D) for _ in range(3)])
            hTs = Slots("hTt", [A.b16(8 * 512) for _ in range(2)])
            sqs = Slots("ssq", [A.f32(2) for _ in range(4)])
            pendB = []

            def stageB():
                for (hb, hbk, hT3, hTk, s) in pendB:
                    bk, bkk, _ = next_bank()
                    b16v = bk.bitcast(BF16)
                    for k in range(8):
                        P.add("pe", lambda e, k=k, b16v=b16v, hb=hb: e.transpose(b16v[:, k * 128:(k + 1) * 128],
                                                                                 hb[:, k * 128:(k + 1) * 128], ident),
                              reads=[hbk, "ident"], writes=[bkk])
                    cp("act" if s % 2 == 0 else "dve", hT3[:, :, s * 128:(s + 1) * 128],
                       b16v.rearrange("p (k t) -> p k t", t=128), [bkk], [(hTk, s)])
                del pendB[:]

            pendStore = []
            for g in range(17):
                nsub = 4 if g < 16 else 2
                hTt, hTk = hTs.next()
                hT3 = hTt.rearrange("p (k t) -> p k t", t=512)
                for s in range(nsub):
                    src = x[g * 512 + s * 128: g * 512 + (s + 1) * 128, :] if g < 16 else ctx[s * 128:(s + 1) * 128, :]
                    Gk, Sk = ("G1", "sh1") if g < 16 else ("cG1", "csh1")
                    Gt, St = (G1, sh1) if g < 16 else (cG1, csh1)
                    xt, xk = xts.next()
                    dma("sp", xt, src, [], [xk], "ph_x%d" % xk[1])
                    sq, sqk = sqs.next()
                    act(junk, xt, AF.Square, [xk], ["junk", sqk], accum_out=sq[:, 0:1])
                    rstd_from_ssq(sq[:, 0:1], 1, 1.0 / D, [sqk], [sqk])
                    hn, hnk = hns.next()
                    stt(hn, xt, sq[:, 0:1], Gt, ALU.mult, ALU.mult, [xk, sqk, Gk], [hnk])
                    hb, hbk = hbs.next()
                    tt("pool", hb, hn, St, ALU.add, [hnk, Sk], [hbk])
                    stageB()
                    for (g_, ns_, hT3_, hTk_) in pendStore:
                        dma("sp", hT[:, :, g_ * 512: g_ * 512 + ns_ * 128], hT3_[:, :, 0:ns_ * 128],
                            [(hTk_, s_) for s_ in range(ns_)], [("hT", g_)], "ph_st%d" % hTk_[1])
                    del pendStore[:]
                    pendB.append((hb, hbk, hT3, hTk, s))
                pendStore.append((g, nsub, hT3, hTk))
            stageB()
            for (g_, ns_, hT3_, hTk_) in pendStore:
                dma("sp", hT[:, :, g_ * 512: g_ * 512 + ns_ * 128], hT3_[:, :, 0:ns_ * 128],
                    [(hTk_, s_) for s_ in range(ns_)], [("hT", g_)], "ph_st%d" % hTk_[1])
            P.barrier()

        def load_w(dst3, src, nk, name, dkey):
            keys = []
            for k in range(nk):
                kk = (name, k)
                dma("pool", dst3[:, k, :], src[k * 128:(k + 1) * 128, :], [], [kk], dkey)
                keys.append(kk)
            return keys

        def load_hT(hts, g, ntok, pfx):
            hTt, hk = hts.next()
            h3 = hTt.rearrange("p (k t) -> p k t", t=512)
            dma("sp", h3[:, :, 0:ntok], hT[:, :, g * 512: g * 512 + ntok], [("hT", g)], [hk], "%s_h%d" % (pfx, hk[1]))
            return h3, hk

        def phase_rope(col0, dst, with_ctx, pfx):
            A.reset()
            W = A.b16(8 * 1024).rearrange("p (k n) -> p k n", n=1024)
            wk = load_w(W, w_in[:, col0:col0 + 1024], 8, pfx + "W", pfx + "_w")
            hts = Slots(pfx + "ht", [A.b16(8 * 512) for _ in range(2)])
            coss = Slots(pfx + "cos", [A.f32(512) for _ in range(2)])
            sins = Slots(pfx + "sin", [A.f32(512) for _ in range(2)])
            t1s = Slots(pfx + "t1", [A.f32(512) for _ in range(2)])
            t2s = Slots(pfx + "t2", [A.f32(512) for _ in range(2)])
            qrs = Slots(pfx + "qr", [A.b16(512) for _ in range(3)])
            outs = Slots(pfx + "out", [A.b16(8 * 512) for _ in range(2)])
            ng = 17 if with_ctx else 16

            def loads(g):
                ntok = 512 if g < 16 else 256
                h3, hk = load_hT(hts, g, ntok, pfx)
                ct = ck = sn = sk = None
                if g < 16:
                    ct, ck = coss.next()
                    sn, sk = sins.next()
                    dma("sp", ct, ropecos[:, g * 512:(g + 1) * 512], [], [ck], "%s_c%d" % (pfx, ck[1]))
                    dma("sp", sn, ropesin[:, g * 512:(g + 1) * 512], [], [sk], "%s_s%d" % (pfx, sk[1]))
                return (h3, hk, ct, ck, sn, sk)

            pendR = []

            def runR():
                for f_ in pendR:
                    f_()
                del pendR[:]

            nxt = loads(0)
            for g in range(ng):
                ntok = 512 if g < 16 else 256
                h3, hk, ct, ck, sn, sk = nxt
                nxt = loads(g + 1) if g + 1 < ng else None
                ot, ok = outs.next()
                o3 = ot.rearrange("p (h t) -> p h t", t=512)
                for h in range(8):
                    bA, kA, _ = next_bank()
                    for k in range(8):
                        mm(bA[:, 0:ntok], W[:, k, h * 128:(h + 1) * 128], h3[:, k, 0:ntok], k == 0, k == 7,
                           [hk] + wk, [kA])
                    if g < 16:
                        qr, qrk = qrs.next()
                        cp("act", qr, bA, [kA], [qrk])
                        runR()

                        def fin(bA=bA, kA=kA, qr=qr, qrk=qrk, ct=ct, ck=ck, sn=sn, sk=sk, o3=o3, ok=ok, h=h):
                            bB, kB, _ = next_bank()
                            mm(bB, Psw, qr, True, True, [qrk, "Psw"], [kB])
                            t1, t1k = t1s.next()
                            t2, t2k = t2s.next()
                            tt("dve", t1, bA, ct, ALU.mult, [kA, ck, qrk], [t1k])
                            tt("dve", t2, bB, sn, ALU.mult, [kB, sk], [t2k])
                            tt("pool", o3[:, h, :], t1, t2, ALU.add, [t1k, t2k], [(ok, h)])

                        pendR.append(fin)
                    else:
                        cp("act", o3[:, h, 0:ntok], bA[:, 0:ntok], [kA], [(ok, h)])
                runR()
                dma("sp", dst[:, :, g * 512: g * 512 + ntok].rearrange("h p t -> p h t"), o3[:, :, 0:ntok],
                    [(ok, h) for h in range(8)], [(pfx + "dst", g)], "%s_st%d" % (pfx, ok[1]))
            P.barrier()

        def phaseV():
            A.reset()
            pfx = "pv"
            W = A.b16(8 * 1024).rearrange("p (k n) -> p k n", n=1024)
            wk = load_w(W, w_in[:, 2048:3072], 8, "pvW", "pv_w")
            hts = Slots("pvht", [A.b16(8 * 512) for _ in range(2)])
            vts = Slots("pvvt", [A.b16(1024) for _ in range(3)])
            nxt = load_hT(hts, 0, 512, pfx)
            for g in range(17):
                nsub = 4 if g < 16 else 2
                h3, hk = nxt
                nxt = load_hT(hts, g + 1, 512 if g + 1 < 16 else 256, pfx) if g + 1 < 17 else None
                for s in range(nsub):
                    pr, pk, _ = next_pair()
                    for half in range(2):
                        for k in range(8):
                            mm(pr[:, half * 512:(half + 1) * 512], h3[:, k, s * 128:(s + 1) * 128],
                               W[:, k, half * 512:(half + 1) * 512], k == 0, k == 7, [hk] + wk, [pk[half]])
                    vt, vk = vts.next()
                    cp("act", vt[:, 0:512], pr[:, 0:512], [pk[0]], [(vk, 0)])
                    cp("dve", vt[:, 512:1024], pr[:, 512:1024], [pk[1]], [(vk, 1)])
                    tok0 = g * 512 + s * 128
                    dma("sp", vtok[tok0:tok0 + 128, :], vt, [(vk, 0), (vk, 1)], [("vtok", tok0 // 128)], "pv_st%d" % vk[1])
            P.barrier()

        def phaseG():
            A.reset()
            pfx = "pg"
            Wgo = A.b16(8 * 1024).rearrange("p (k n) -> p k n", n=1024)
            Wmg = A.b16(8 * 2048).rearrange("p (k n) -> p k n", n=2048)
            wgk = load_w(Wgo, w_in[:, 5152:6176], 8, "pgWgo", "pg_w1")
            wmk = load_w(Wmg, w_in[:, 6176:8224], 8, "pgWmg", "pg_w2")
            hts = Slots("pght", [A.b16(8 * 512) for _ in range(2)])
            sgs = Slots("pgsg", [A.f32(1024) for _ in range(2)])
            gos = Slots("pggo", [A.b16(1024) for _ in range(2)])
            mgs = Slots("pgmg", [A.b16(4 * 512) for _ in range(2)])
            nxt = load_hT(hts, 0, 512, pfx)
            for g in range(16):
                h3, hk = nxt
                nxt = load_hT(hts, g + 1, 512, pfx) if g + 1 < 16 else None
                for s in range(4):
                    pr, pk, _ = next_pair()
                    for half in range(2):
                        for k in range(8):
                            mm(pr[:, half * 512:(half + 1) * 512], h3[:, k, s * 128:(s + 1) * 128],
                               Wgo[:, k, half * 512:(half + 1) * 512], k == 0, k == 7, [hk] + wgk, [pk[half]])
                    sg, sgk = sgs.next()
                    act(sg, pr, AF.Sigmoid, pk, [sgk])
                    go, gok = gos.next()
                    tt("dve", go, pr, sg, ALU.mult, pk + [sgk], [gok])
                    tok0 = g * 512 + s * 128
                    dma("sp", gout[tok0:tok0 + 128, :], go, [gok], [("gout", tok0 // 128)], "pg_st%d" % gok[1])
                for f4 in range(4):
                    mg, mgk = mgs.next()
                    m3 = mg.rearrange("p (f t) -> p f t", t=512)
                    for fi in range(4):
                        f = f4 * 4 + fi
                        bA, kA, _ = next_bank()
                        for k in range(8):
                            mm(bA, Wmg[:, k, f * 128:(f + 1) * 128], h3[:, k, :], k == 0, k == 7, [hk] + wmk, [kA])
                        act(m3[:, fi, :], bA, AF.Sigmoid, [kA], [(mgk, fi)])
                    dma("sp", gatesT[f4 * 4:(f4 + 1) * 4, :, g * 512:(g + 1) * 512].rearrange("f p t -> p f t"), m3,
                        [(mgk, fi) for fi in range(4)], [("gatesT", g, f4)], "pg_sm%d" % mgk[1])
            P.barrier()

        def phaseGLAprep():
            A.reset()
            pfx = "pa"
            Wq = A.b16(8 * 512).rearrange("p (k n) -> p k n", n=512)
            Wk = A.b16(8 * 512).rearrange("p (k n) -> p k n", n=512)
            Wv = A.b16(8 * 1024).rearrange("p (k n) -> p k n", n=1024)
            Wl = A.b16(8 * 32).rearrange("p (k n) -> p k n", n=32)
            wqk = load_w(Wq, w_in[:, 3072:3584], 8, "paWq", "pa_w1")
            wkk = load_w(Wk, w_in[:, 3584:4096], 8, "paWk", "pa_w2")
            wvk = load_w(Wv, w_in[:, 4096:5120], 8, "paWv", "pa_w3")
            wlk = load_w(Wl, w_in[:, 5120:5152], 8, "paWl", "pa_w4")
            WGU = A.b16(2 * 512)
            WGU3 = WGU.rearrange("p (z n) -> p z n", n=512)
            dma("pool", WGU3[0:17, :, :], wgu.rearrange("z r n -> r z n"), [], ["WGU"], "pa_wgu")
            tri = {}
            for nm, stp, cm, bs in (("TriF", -1, 1, 0), ("MF", 1, -1, 1), ("TriB", 1, -1, 0), ("MB", -1, 1, 1)):
                tl = A.f32(128)
                P.add("pool", lambda e, tl=tl, stp=stp, cm=cm, bs=bs: e.affine_select(
                    out=tl, in_=zer[:, 0:128], pattern=[[stp, 128]], compare_op=ALU.is_gt, fill=-1.0 / 16, base=bs,
                    channel_multiplier=cm), reads=["zer"], writes=[nm])
                tri[nm] = tl
            hts = Slots("paht", [A.b16(8 * 512) for _ in range(2)])
            gls = [Slots("pagl%d" % z, [A.b16(512) for _ in range(2)]) for z in range(2)]
            for z in range(2):
                for i, a_ in enumerate(gls[z].aps):
                    memset("pool", a_[0:32, :], 1.0, [("pagl%d" % z, i)])
            Ls = [A.f32(4 * 512) for _ in range(2)]
            tmps = Slots("patmp", [A.f32(512) for _ in range(2)])
            eks = Slots("paek", [A.f32(512) for _ in range(2)])
            khs = Slots("pakh", [A.b16(512) for _ in range(3)])
            vts = Slots("pavt", [A.b16(1024) for _ in range(2)])
            ebs = Slots("paeb", [A.f32(512) for _ in range(2)])
            ens = Slots("paen", [A.f32(512) for _ in range(2)])
            qks = {}
            for nm in ("QF", "KF", "QB", "KB"):
                qks[nm] = Slots("pa" + nm, [A.b16(4 * 512) for _ in range(2)])
            dsts = {"QF": QF, "KF": KF, "QB": QB, "KB": KB}
            DEC = [DECF.rearrange("p (c h) -> p c h", h=4), DECB.rearrange("p (c h) -> p c h", h=4)]
            nxt = load_hT(hts, 0, 512, pfx)
            for g in range(17):
                nsub = 4 if g < 16 else 2
                ntok = nsub * 128
                h3, hk = nxt
                nxt = load_hT(hts, g + 1, 512 if g + 1 < 16 else 256, pfx) if g + 1 < 17 else None
                glz = []
                for z in range(2):
                    bA, kA, _ = next_bank()
                    for k in range(8):
                        mm(bA[0:16, 0:ntok], Wl[:, k, z * 16:(z + 1) * 16], h3[:, k, 0:ntok], k == 0, k == 7, [hk] + wlk, [kA])
                    gl, glk = gls[z].next()
                    cp("dve", gl[0:16, 0:ntok], bA[0:16, 0:ntok], [kA], [glk])
                    glz.append((gl, glk))
                L3 = [Ls[z].rearrange("p (s n) -> p s n", n=512) for z in range(2)]
                for s in range(nsub):
                    tok0 = g * 512 + s * 128
                    bK, kK, _ = next_bank()
                    for k in range(8):
                        mm(bK, h3[:, k, s * 128:(s + 1) * 128], Wk[:, k, :], k == 0, k == 7, [hk] + wkk, [kK])
                    for z in range(2):
                        gl, glk = glz[z]
                        bG, kG, _ = next_bank()
                        mm(bG, gl[0:17, s * 128:(s + 1) * 128], WGU3[0:17, z, :], True, True, [glk, "WGU"], [kG])
                        tm, tmk = tmps.next()
                        act(tm, bG, AF.Exp, [kG], [tmk], scale=-1.0)
                        act(L3[z][:, s, :], tm, AF.Ln, [tmk, "c_one"], [("paL", z, s)], bias=c_one[:, 0:1])
                        bE, kE, _ = next_bank()
                        mm(bE, tri["MF" if z == 0 else "MB"], L3[z][:, s, :], True, True,
                           [("paL", z, s), "MF", "MB"], [kE])
                        ek, ekk = eks.next()
                        act(ek, bE, AF.Exp, [kE], [ekk])
                        kh, khk = khs.next()
                        tt("dve", kh, bK, ek, ALU.mult, [kK, ekk], [khk])
                        dma("sp", (KHF if z == 0 else KHB)[tok0:tok0 + 128, :], kh, [khk],
                            [("KH", z, tok0 // 128)], "pa_kh%d" % khk[1])
                    pr, pk, _ = next_pair()
                    for half in range(2):
                        for k in range(8):
                            mm(pr[:, half * 512:(half + 1) * 512], h3[:, k, s * 128:(s + 1) * 128],
                               Wv[:, k, half * 512:(half + 1) * 512], k == 0, k == 7, [hk] + wvk, [pk[half]])
                    vt, vk = vts.next()
                    cp("act", vt[:, 0:512], pr[:, 0:512], [pk[0]], [(vk, 0)])
                    cp("dve", vt[:, 512:1024], pr[:, 512:1024], [pk[1]], [(vk, 1)])
                    dma("sp", GV[tok0:tok0 + 128, :], vt, [(vk, 0), (vk, 1)], [("GV", tok0 // 128)], "pa_gv%d" % vk[1])
                tl = {}
                for nm in ("QF", "KF", "QB", "KB"):
                    t_, k_ = qks[nm].next()
                    tl[nm] = (t_.rearrange("p (h t) -> p h t", t=512), k_)
                for h in range(4):
                    bQ, kQ, _ = next_bank()
                    for k in range(8):
                        mm(bQ[:, 0:ntok], Wq[:, k, h * 128:(h + 1) * 128], h3[:, k, 0:ntok], k == 0, k == 7, [hk] + wqk, [kQ])
                    bKt, kKt, _ = next_bank()
                    for k in range(8):
                        mm(bKt[:, 0:ntok], Wk[:, k, h * 128:(h + 1) * 128], h3[:, k, 0:ntok], k == 0, k == 7, [hk] + wkk, [kKt])
                    for z in range(2):
                        bT, kT_, _ = next_bank()
                        for s in range(nsub):
                            mm(bT[:, s * 128:(s + 1) * 128], L3[z][:, s, h * 128:(h + 1) * 128],
                               tri["TriF" if z == 0 else "TriB"], True, True, [("paL", z, s), "TriF", "TriB"], [kT_])
                        eb, ebk = ebs.next()
                        en, enk = ens.next()
                        act(eb[:, 0:ntok], bT[:, 0:ntok], AF.Exp, [kT_], [ebk])
                        act(en[:, 0:ntok], bT[:, 0:ntok], AF.Exp, [kT_], [enk], scale=-1.0)
                        qn, kn = ("QF", "KF") if z == 0 else ("QB", "KB")
                        stt(tl[qn][0][:, h, 0:ntok], bQ[:, 0:ntok], 128.0 ** -0.5, eb[:, 0:ntok], ALU.mult, ALU.mult,
                            [kQ, ebk], [(tl[qn][1], h)])
                        tt("dve", tl[kn][0][:, h, 0:ntok], bKt[:, 0:ntok], en[:, 0:ntok], ALU.mult, [kKt, enk], [(tl[kn][1], h)])
                        col = 127 if z == 0 else 0
                        ebv = eb.rearrange("p (s t) -> p s t", t=128)
                        cp("pool", DEC[z][:, g * 4:g * 4 + nsub, h], ebv[:, 0:nsub, col], [ebk], [("DEC", z, g, h)])
                for nm in ("QF", "KF", "QB", "KB"):
                    t3, k_ = tl[nm]
                    dma("sp", dsts[nm][:, :, g * 512:g * 512 + ntok].rearrange("h p t -> p h t"), t3[:, :, 0:ntok],
                        [(k_, h) for h in range(4)], [(nm, g)], "pa_%s%d" % (nm, k_[1]))
            P.barrier()

        def phaseAttn():
            A.reset()
            NKT = TT // 128
            Ks = Slots("abK", [A.b16(TT) for _ in range(2)])
            Vs = Slots("abV", [A.b16(NKT * 129) for _ in range(2)])
            for i, v_ in enumerate(Vs.aps):
                memset("pool", v_, 1.0, [("abV", i)])
            Qs = Slots("abQ", [A.b16(512) for _ in range(3)])
            Es = Slots("abE", [A.b16(1024) for _ in range(3)])
            sm = Slots("absm", [A.f32(16) for _ in range(2)])
            tms = Slots("abt", [A.f32(128) for _ in range(2)])
            os_ = Slots("abo", [A.f32(128) for _ in range(2)])
            jk = A.f32(128)
            ybs = Slots("aby", [A.b16(128) for _ in range(8)])
            yTs = Slots("abyT", [A.b16(512) for _ in range(3)])
            spair = [0]
            acc0 = bank(4)
            acc1 = bank(5)
            acc2 = bank(6)

            def accreg(c, qs):
                if qs < 3:
                    b = acc0 if c == 0 else acc1
                    return b[:, qs * 129:(qs + 1) * 129], ("ps", 4 + c)
                return acc2[:, c * 129:(c + 1) * 129], ("ps", 6)

            accsb = Slots("abacc", [A.f32(3 * 512) for _ in range(2)])
            heads = {}

            def load_head(h):
                Kt, Kk = Ks.next()
                Vt, Vk = Vs.next()
                V3 = Vt.rearrange("p (n e) -> p n e", e=129)
                dma("sp", Kt, kT[h, :, :], [("pkdst", g) for g in range(17)], [Kk], "ab_k%d" % Kk[1])
                vsrc = vtok[:, h * 128:(h + 1) * 128].rearrange("(n p) e -> p n e", p=128)
                for j in range(3):
                    dma("sp", V3[:, j * 22:(j + 1) * 22, 0:128], vsrc[:, j * 22:(j + 1) * 22, :],
                        [("vtok", n) for n in range(j * 22, (j + 1) * 22)] + [Vk], [(Vk, "d", j)], "ab_v%d_%d" % (Vk[1], j))
                heads[h] = (Kt, Kk, V3, [Vk, (Vk, "d", 0), (Vk, "d", 1), (Vk, "d", 2)])

            qtiles = {}

            def load_q(h, qb):
                Qt, Qk = Qs.next()
                dma("sp", Qt, qT[h, :, qb * 512:(qb + 1) * 512], [("pqdst", qb)], [Qk], "ab_q%d" % Qk[1])
                qtiles[(h, qb)] = (Qt, Qk)

            iters = [(h, qb, kt) for h in range(8) for qb in range(16) for kt in range(NKT)]
            Sbuf = {}

            def emit_S(i):
                h, qb, kt = iters[i]
                Kt, Kk, V3, Vkeys = heads[h]
                Qt, Qk = qtiles[(h, qb)]
                sp_ = i % 2
                S = ps[:, sp_ * 1024:(sp_ + 1) * 1024]
                Sk = [("ps", 2 * sp_), ("ps", 2 * sp_ + 1)]
                for c in range(2):
                    mm(S[:, c * 512:(c + 1) * 512], Kt[c * 64:(c + 1) * 64, kt * 128:(kt + 1) * 128],
                       Qt[c * 64:(c + 1) * 64, :], True, True, [Kk, Qk], [Sk[c]])
                Sbuf[i] = (S, Sk)

            def finish(h, qb):
                acck = [("ps", 4), ("ps", 5), ("ps", 6)]
                ab, abk = accsb.next()
                for j in range(3):
                    n_ = 387 if j < 2 else 258
                    cp("dve", ab[:, j * 512:j * 512 + n_], bank(4 + j)[:, 0:n_], [acck[j]], [(abk, j)])
                abks = [(abk, j) for j in range(3)]

                def sreg(c, qs):
                    if qs < 3:
                        return ab[:, c * 512 + qs * 129: c * 512 + (qs + 1) * 129]
                    return ab[:, 1024 + c * 129: 1024 + (c + 1) * 129]

                s_, sk_ = sm.next()
                for c in range(2):
                    for qs in range(4):
                        reg = sreg(c, qs)
                        P.add("dve", lambda e, o_=s_[:, c * 4 + qs: c * 4 + qs + 1], i_=reg[:, 128:129]: e.reciprocal(out=o_, in_=i_),
                              reads=abks, writes=[(sk_, c * 4 + qs)])
                ts("dve", s_[:, 4:8], s_[:, 4:8], lamcol[:, 0:1], None, ALU.mult, ALU.bypass,
                   [(sk_, j) for j in range(4, 8)] + ["lamcol"], [(sk_, "w")])
                yT, yTk = yTs.next()
                for qs in range(4):
                    r0 = sreg(0, qs)
                    r1 = sreg(1, qs)
                    tm, tmk = tms.next()
                    ts("dve", tm, r1[:, 0:128], s_[:, 4 + qs:5 + qs], None, ALU.mult, ALU.bypass,
                       abks + [(sk_, "w")], [tmk])
                    o_, ok_ = os_.next()
                    stt(o_, r0[:, 0:128], s_[:, qs:qs + 1], tm, ALU.mult, ALU.subtract,
                        abks + [(sk_, qs), tmk], [ok_])
                    stt(jk, o_, 1.0, o_, ALU.mult, ALU.mult, [ok_], ["abjk", (sk_, 8 + qs)],
                        accum_out=s_[:, 8 + qs:9 + qs])
                    rstd_from_ssq(s_[:, 8 + qs:9 + qs], 1, 1.0 / 128, [(sk_, 8 + qs)], [(sk_, 8 + qs)])
                    yb, ybk = ybs.next()
                    stt(yb, o_, s_[:, 8 + qs:9 + qs], hnda, ALU.mult, ALU.mult, [ok_, (sk_, 8 + qs), "hnda"], [ybk])
                    pend_tr.append((qs, yb, ybk, yT, yTk, h, qb))

            pend_tr = []

            def flush_tr():
                if not pend_tr:
                    return
                b7 = bank(7).bitcast(BF16)
                for (qs, yb, ybk, yT, yTk, h, qb) in pend_tr:
                    P.add("pe", lambda e, o2=b7[:, qs * 128:(qs + 1) * 128], i2=yb: e.transpose(o2, i2, ident),
                          reads=[ybk, "ident"], writes=[("ps", 7)])
                (qs, yb, ybk, yT, yTk, h, qb) = pend_tr[0]
                cp("dve", yT, b7[:, 0:512], [("ps", 7)], [yTk])
                dma("sp", ydaT[h, :, qb * 512:(qb + 1) * 512], yT, [yTk], [("ydaT", h, qb)], "ab_y%d" % yTk[1])
                del pend_tr[:]

            load_head(0)
            load_q(0, 0)
            nI = len(iters)

            def prep_S(j):
                if j >= nI:
                    return
                hn_, qbn_, ktn_ = iters[j]
                if hn_ not in heads:
                    load_head(hn_)
                if (hn_, qbn_) not in qtiles:
                    load_q(hn_, qbn_)
                emit_S(j)

            prep_S(0)
            prep_S(1)
            for i in range(nI):
                h, qb, kt = iters[i]
                if kt == 0:
                    nqb = (h, qb + 1) if qb < 15 else ((h + 1, 0) if h < 7 else None)
                    if nqb is not None and nqb not in qtiles:
                        load_q(*nqb)
                    if qb == 1 and h < 7 and (h + 1) not in heads:
                        load_head(h + 1)
                S, Sk = Sbuf.pop(i)
                Kt, Kk, V3, Vkeys = heads[h]
                E, Ek = Es.next()
                act(E, S, AF.Exp, Sk, [Ek], scale=0.125)
                prep_S(i + 2)
                for c in range(2):
                    for qs in range(4):
                        reg, rk = accreg(c, qs)
                        first = (kt == 0) and ((qs == 0) or (qs == 3 and c == 0))
                        mm(reg, E[:, c * 512 + qs * 128: c * 512 + (qs + 1) * 128], V3[:, kt, :],
                           first, kt == NKT - 1, [Ek] + Vkeys, [rk], skip_group_check=True)
                if kt == 8:
                    flush_tr()
                if kt == NKT - 1:
                    finish(h, qb)
            flush_tr()
            P.barrier()

        def phaseGLA():
            A.reset()
            C = {}
            for z in range(2):
                pfx = "gf" if z == 0 else "gb"
                c = C[z] = {}
                c["pfx"] = pfx
                c["S32"] = A.f32(4 * 256)
                c["S16"] = A.b16(4 * 256)
                memset("pool", c["S32"], 0.0, [(pfx + "S32", h) for h in range(4)])
                memset("pool", c["S16"], 0.0, [(pfx + "S16", h) for h in range(4)])
                c["q"] = Slots(pfx + "q", [A.b16(512) for _ in range(3)])
                c["k"] = Slots(pfx + "k", [A.b16(512) for _ in range(3)])
                c["kh"] = Slots(pfx + "kh", [A.b16(512) for _ in range(3)])
                c["v"] = Slots(pfx + "v", [A.b16(1024) for _ in range(3)])
                c["am"] = Slots(pfx + "am", [A.b16(512) for _ in range(2)])
                c["ob"] = Slots(pfx + "ob", [A.f32(1024) for _ in range(3)])
                c["of"] = Slots(pfx + "of", [A.f32(1024) for _ in range(3)])
                c["go"] = Slots(pfx + "go", [A.b16(1024) for _ in range(3)])
                c["sq"] = Slots(pfx + "sq", [A.f32(4) for _ in range(3)])
                c["yb"] = Slots(pfx + "yb", [A.b16(1024) for _ in range(3)])
                c["yT"] = Slots(pfx + "yT", [A.b16(8 * 128) for _ in range(2)])
                c["jk"] = A.f32(256)
                c["order"] = [64, 65] + list(range(64)) if z == 0 else [65, 64] + list(range(63, -1, -1))
            visited = set()
            gpre = {}

            def gla_loads(z, n):
                c = C[z]
                pfx = c["pfx"]
                Qd, Kd, KHd = (QF, KF, KHF) if z == 0 else (QB, KB, KHB)
                Qn, Kn = ("QF", "KF") if z == 0 else ("QB", "KB")
                t0 = n * 128
                g = n // 4
                kh, khk = c["kh"].next()
                vt, vk = c["v"].next()
                q3 = qk = k3 = kk = None
                if n < 64:
                    qt, qk = c["q"].next()
                    kt_, kk = c["k"].next()
                    q3 = qt.rearrange("p (h t) -> p h t", t=128)
                    k3 = kt_.rearrange("p (h t) -> p h t", t=128)
                    dma("sp", q3, Qd[:, :, t0:t0 + 128].rearrange("h p t -> p h t"), [(Qn, g)], [qk], "%s_q%d" % (pfx, qk[1]))
                    dma("sp", k3, Kd[:, :, t0:t0 + 128].rearrange("h p t -> p h t"), [(Kn, g)], [kk], "%s_k%d" % (pfx, kk[1]))
                dma("sp", kh, KHd[t0:t0 + 128, :], [("KH", z, n)], [khk], "%s_kh%d" % (pfx, khk[1]))
                dma("sp", vt, GV[t0:t0 + 128, :], [("GV", n)], [vk], "%s_v%d" % (pfx, vk[1]))
                gpre[(z, n)] = (kh, khk, vt, vk, q3, qk, k3, kk)

            def stage_a(z, n):
                c = C[z]
                pfx = c["pfx"]
                X = {"z": z, "n": n, "c": c, "pfx": pfx}
                t0 = X["t0"] = n * 128
                X["is_ctx"] = n >= 64
                X["second"] = n in visited
                visited.add(n)
                if (z, n) not in gpre:
                    gla_loads(z, n)
                (kh, khk, vt, vk, q3, qk, k3, kk) = gpre.pop((z, n))
                X.update(kh=kh, khk=khk, vt=vt, vk=vk, q3=q3, qk=qk, k3=k3, kk=kk)
                if not X["is_ctx"]:
                    if X["second"]:
                        oft, ofk = c["of"].next()
                        dma("sp", oft, o_f[t0:t0 + 128, :], [("o_f", n)], [ofk], "%s_of%d" % (pfx, ofk[1]))
                        got, gok = c["go"].next()
                        dma("sp", got, gout[t0:t0 + 128, :], [("gout", n)], [gok], "%s_go%d" % (pfx, gok[1]))
                        X.update(oft=oft, ofk=ofk, got=got, gok=gok)
                    bA, kA, _ = next_bank()
                    for h in range(4):
                        mm(bA[:, h * 128:(h + 1) * 128], k3[:, h, :], q3[:, h, :], True, True, [kk, qk], [kA])
                    X.update(bA=bA, kA=kA)
                pu = [next_bank(), next_bank()]
                for h in range(4):
                    ub, ubk, _ = pu[h // 2]
                    ureg = ub[:, (h % 2) * 256:(h % 2 + 1) * 256]
                    mm(ureg, kh[:, h * 128:(h + 1) * 128], vt[:, h * 256:(h + 1) * 256], True, True, [khk, vk], [ubk])
                X["pu"] = pu
                return X

            def stage_b(X):
                c, z, n, pfx = X["c"], X["z"], X["n"], X["pfx"]
                mask = maskF if z == 0 else maskB
                maskk = "maskF" if z == 0 else "maskB"
                DEC = (DECF if z == 0 else DECB).rearrange("p (c h) -> p c h", h=4)
                S32 = c["S32"]
                if not X["is_ctx"]:
                    am, amk = c["am"].next()
                    tt("dve", am, X["bA"], mask, ALU.mult, [X["kA"], maskk], [amk])
                    X.update(am=am, amk=amk)
                for h in range(4):
                    ub, ubk, _ = X["pu"][h // 2]
                    ureg = ub[:, (h % 2) * 256:(h % 2 + 1) * 256]
                    stt(S32[:, h * 256:(h + 1) * 256], S32[:, h * 256:(h + 1) * 256], DEC[:, n, h:h + 1], ureg,
                        ALU.mult, ALU.add, [(pfx + "S32", h), ubk] + [("DEC", z, n // 4, h)], [(pfx + "S32", h)])

            def stage_c(X):
                if X["is_ctx"]:
                    return
                c, pfx = X["c"], X["pfx"]
                S16 = c["S16"]
                q3, qk, vt, vk, am, amk = X["q3"], X["qk"], X["vt"], X["vk"], X["am"], X["amk"]
                pr = [next_bank(), next_bank()]
                for h in range(4):
                    ob, obk, _ = pr[h // 2]
                    oreg = ob[:, (h % 2) * 256:(h % 2 + 1) * 256]
                    mm(oreg, q3[:, h, :], S16[:, h * 256:(h + 1) * 256], True, False, [qk, (pfx + "S16", h)], [obk])
                    mm(oreg, am[:, h * 128:(h + 1) * 128], vt[:, h * 256:(h + 1) * 256], False, True, [amk, vk], [obk])
                X["pr"] = pr

            def stage_d(X):
                c, z, n, pfx, t0 = X["c"], X["z"], X["n"], X["pfx"], X["t0"]
                S32, S16 = c["S32"], c["S16"]
                for h in range(4):
                    cp("act", S16[:, h * 256:(h + 1) * 256], S32[:, h * 256:(h + 1) * 256], [(pfx + "S32", h)], [(pfx + "S16", h)])
                if X["is_ctx"]:
                    return
                pr = X["pr"]
                ot, otk = c["ob"].next()
                cp("act", ot[:, 0:512], pr[0][0], [pr[0][1]], [(otk, 0)])
                cp("act", ot[:, 512:1024], pr[1][0], [pr[1][1]], [(otk, 1)])
                if not X["second"]:
                    dma("sp", o_f[t0:t0 + 128, :], ot, [(otk, 0), (otk, 1)], [("o_f", n)], "%s_st%d" % (pfx, otk[1]))
                    return
                oft, ofk, got, gok = X["oft"], X["ofk"], X["got"], X["gok"]

                def fin(c=c, pfx=pfx, z=z, n=n, t0=t0, ot=ot, otk=otk, oft=oft, ofk=ofk, got=got, gok=gok):
                    osm, osk = ot, otk
                    tt("dve", osm[:, 0:512], ot[:, 0:512], oft[:, 0:512], ALU.add, [(otk, 0), ofk], [(osk, 0)])
                    tt("pool", osm[:, 512:1024], ot[:, 512:1024], oft[:, 512:1024], ALU.add, [(otk, 1), ofk], [(osk, 1)])
                    sq, sqk = c["sq"].next()
                    jk = c["jk"]
                    for h in range(4):
                        act(jk, osm[:, h * 256:(h + 1) * 256], AF.Square, [(osk, h // 2)], [pfx + "jk", (sqk, h)],
                            accum_out=sq[:, h:h + 1])
                    rstd_from_ssq(sq[:, 0:4], 4, 1.0 / 256, [(sqk, h) for h in range(4)], [(sqk, "r")])
                    yt, ytk = oft, ofk
                    yb, ybk = c["yb"].next()
                    for h in range(4):
                        stt(yt[:, h * 256:(h + 1) * 256], osm[:, h * 256:(h + 1) * 256], sq[:, h:h + 1], hngla, ALU.mult, ALU.mult,
                            [(osk, h // 2), (sqk, "r"), "hngla", ytk], [(ytk, "y", h)])
                    tt("pool", yb, yt, got, ALU.mult, [(ytk, "y", h) for h in range(4)] + [gok], [ybk])
                    pend.append((z, yb, ybk, t0, n))

                pend_fin.append(fin)

            def step_pair(n0, n1):
                Xa = stage_a(0, n0)
                Xb = stage_a(1, n1)
                stage_b(Xa)
                stage_b(Xb)
                stage_c(Xa)
                stage_c(Xb)
                stage_d(Xa)
                stage_d(Xb)

            pend_fin = []

            def run_fin():
                for f_ in pend_fin:
                    f_()
                del pend_fin[:]

            pend = []

            def flush():
                for (z, yb, ybk, t0, n) in pend:
                    c = C[z]
                    pfx = c["pfx"]
                    bk, bkk, _ = next_bank()
                    b16v = bk.bitcast(BF16)
                    for k in range(8):
                        P.add("pe", lambda e, k=k, b16v=b16v, yb=yb: e.transpose(b16v[:, k * 128:(k + 1) * 128],
                                                                                 yb[:, k * 128:(k + 1) * 128], ident),
                              reads=[ybk, "ident"], writes=[bkk])
                    yT, yTk = c["yT"].next()
                    cp("act", yT, b16v, [bkk], [yTk])
                    dma("sp", yglaT[:, :, t0:t0 + 128].rearrange("k p t -> p k t"), yT.rearrange("p (k t) -> p k t", t=128),
                        [yTk], [("yglaT", n)], "%s_sy%d" % (pfx, yTk[1]))
                del pend[:]

            gla_loads(0, C[0]["order"][0])
            gla_loads(1, C[1]["order"][0])
            for idx in range(66):
                old = list(pend)
                del pend[:]
                if idx + 1 < 66:
                    gla_loads(0, C[0]["order"][idx + 1])
                    gla_loads(1, C[1]["order"][idx + 1])
                oldf = list(pend_fin)
                del pend_fin[:]
                step_pair(C[0]["order"][idx], C[1]["order"][idx])
                newf = list(pend_fin)
                del pend_fin[:]
                pend_fin.extend(oldf)
                run_fin()
                pend_fin.extend(newf)
                new_ = list(pend)
                del pend[:]
                pend.extend(old)
                flush()
                pend.extend(new_)
            run_fin()
            flush()
            flush()
            P.barrier()

        def phaseMerge():
            A.reset()
            pfx = "pm"
            Wda = A.b16(8 * 1024).rearrange("p (k n) -> p k n", n=1024)
            Wgl = A.b16(8 * 1024).rearrange("p (k n) -> p k n", n=1024)
            Wo = A.b16(8 * 1024).rearrange("p (k n) -> p k n", n=1024)
            wdk = load_w(Wda, w_bda, 8, "pmWda", "pm_w1")
            wgk = load_w(Wgl, w_bgla, 8, "pmWgl", "pm_w2")
            wok = load_w(Wo, w_out, 8, "pmWo", "pm_w3")
            PG1 = A.f32(D); sh2 = A.f32(D); G2 = A.f32(D)
            load_rows(PG1, 2, "PG1", "pm_r0")
            load_rows(sh2, 3, "sh2", "pm_r1")
            load_rows(G2, 4, "G2", "pm_r2")
            yds = Slots("pmyd", [A.b16(8 * 512) for _ in range(2)])
            ygs = Slots("pmyg", [A.b16(8 * 512) for _ in range(2)])
            gts = Slots("pmgt", [A.b16(2 * 512) for _ in range(3)])
            t1s = Slots("pmt1", [A.f32(512) for _ in range(2)])
            t2s = Slots("pmt2", [A.f32(512) for _ in range(2)])
            yT = A.b16(8 * 512)
            yT3 = yT.rearrange("p (k t) -> p k t", t=512)
            xts = Slots("pmx", [A.f32(D) for _ in range(3)])
            zs = Slots("pmz", [A.f32(D) for _ in range(3)])
            hns = Slots("pmhn", [A.f32(D) for _ in range(2)])
            hbs = Slots("pmhb", [A.b16(D) for _ in range(4)])
            junk = A.b16(D)
            sqs = Slots("pmsq", [A.f32(4) for _ in range(6)])
            h2s = Slots("pmh2", [A.b16(8 * 512) for _ in range(2)])
            mpend = []

            def flush_m(keep=0):
                n_ = max(len(mpend) - keep, 0)
                for (hb, hbk, h23, h2k, s) in mpend[:n_]:
                    bk, bkk, _ = next_bank()
                    b16v = bk.bitcast(BF16)
                    for k in range(8):
                        P.add("pe", lambda e, k=k, b16v=b16v, hb=hb: e.transpose(b16v[:, k * 128:(k + 1) * 128],
                                                                                 hb[:, k * 128:(k + 1) * 128], ident),
                              reads=[hbk, "ident"], writes=[bkk])
                    cp("act", h23[:, :, s * 128:(s + 1) * 128], b16v.rearrange("p (k t) -> p k t", t=128), [bkk], [(h2k, s)])
                del mpend[:n_]

            yTs = [yT, A.b16(8 * 512)]

            def step1(g):
                yTg = yTs[g % 2].rearrange("p (k t) -> p k t", t=512)
                yd, ydk = yds.next()
                yg, ygk = ygs.next()
                yd3 = yd.rearrange("p (k t) -> p k t", t=512)
                yg3 = yg.rearrange("p (k t) -> p k t", t=512)
                dma("sp", yd3, ydaT[:, :, g * 512:(g + 1) * 512].rearrange("k p t -> p k t"),
                    [("ydaT", h, g) for h in range(8)], [ydk], "pm_yd%d" % ydk[1])
                dma("sp", yg3, yglaT[:, :, g * 512:(g + 1) * 512].rearrange("k p t -> p k t"),
                    [("yglaT", g * 4 + s) for s in range(4)], [ygk], "pm_yg%d" % ygk[1])
                for fo in range(8):
                    gt, gtk = gts.next()
                    gt3 = gt.rearrange("p (b t) -> p b t", t=512)
                    dma("sp", gt3[:, 0, :], gatesT[fo, :, g * 512:(g + 1) * 512], [("gatesT", g, fo // 4)], [(gtk, 0)],
                        "pm_ga%d" % gtk[1])
                    dma("sp", gt3[:, 1, :], gatesT[8 + fo, :, g * 512:(g + 1) * 512], [("gatesT", g, 2 + fo // 4)], [(gtk, 1)],
                        "pm_gb%d" % gtk[1])
                    bA, kA, _ = next_bank()
                    for k in range(8):
                        mm(bA, Wda[:, k, fo * 128:(fo + 1) * 128], yd3[:, k, :], k == 0, k == 7, [ydk] + wdk, [kA])
                    bB, kB, _ = next_bank()
                    for k in range(8):
                        mm(bB, Wgl[:, k, fo * 128:(fo + 1) * 128], yg3[:, k, :], k == 0, k == 7, [ygk] + wgk, [kB])
                    t1, t1k = t1s.next()
                    t2, t2k = t2s.next()
                    tt("dve", t1, bA, gt3[:, 0, :], ALU.mult, [kA, (gtk, 0)], [t1k])
                    tt("dve", t2, bB, gt3[:, 1, :], ALU.mult, [kB, (gtk, 1)], [t2k])
                    tt("pool", yTg[:, fo, :], t1, t2, ALU.add, [t1k, t2k], [("pmyT", g % 2, fo)])

            def step2(g):
                yTg = yTs[g % 2].rearrange("p (k t) -> p k t", t=512)
                h2t, h2k = h2s.next()
                h23 = h2t.rearrange("p (k t) -> p k t", t=512)
                for s in range(4):
                    tok0 = g * 512 + s * 128
                    if (g, s) not in xpre:
                        ldx(g, s)
                    xt, xk = xpre.pop((g, s))
                    ns_ = (g, s + 1) if s < 3 else (g + 1, 0)
                    if ns_[0] < 16:
                        ldx(*ns_)
                    pr, pk, _ = next_pair()
                    for half in range(2):
                        for k in range(8):
                            mm(pr[:, half * 512:(half + 1) * 512], yTg[:, k, s * 128:(s + 1) * 128],
                               Wo[:, k, half * 512:(half + 1) * 512], k == 0, k == 7, [("pmyT", g % 2, k)] + wok, [pk[half]])
                    flush_m(1)
                    sq, sqk = sqs.next()
                    act(junk, pr, AF.Square, pk, ["pmjunk", (sqk, 0)], accum_out=sq[:, 0:1])
                    rstd_from_ssq(sq[:, 0:1], 1, 1.0 / D, [(sqk, 0)], [(sqk, 0)])
                    zt, zk = zs.next()
                    stt(zt, pr, sq[:, 0:1], PG1, ALU.mult, ALU.mult, pk + [(sqk, 0), "PG1"], [(zk, "a")])
                    xn, xnk = zt, (zk, "b")
                    tt("pool", xn, zt, xt, ALU.add, [(zk, "a"), xk], [xnk])
                    dma("sp", xnew[tok0:tok0 + 128, :], xn, [xnk], [("xnew", tok0 // 128)], "pm_sx%d" % zk[1])
                    runB()

                    def stageB(xn=xn, xnk=xnk, sq=sq, sqk=sqk, h23=h23, h2k=h2k, s=s):
                        act(junk2, xn, AF.Square, [xnk], ["pmjunk", (sqk, 1)], accum_out=sq[:, 1:2])
                        rstd_from_ssq(sq[:, 1:2], 1, 1.0 / D, [(sqk, 1)], [(sqk, 1)])
                        hn, hnk = hns.next()
                        stt(hn, xn, sq[:, 1:2], G2, ALU.mult, ALU.mult, [xnk, (sqk, 1), "G2"], [hnk])
                        hb, hbk = hbs.next()
                        tt("dve", hb, hn, sh2, ALU.add, [hnk, "sh2"], [hbk])
                        mpend.append((hb, hbk, h23, h2k, s))

                    pendB.append(stageB)
                pend_store.append((g, h23, h2k))

            pend_store = []
            xpre = {}
            pendB = []
            junk2 = junk

            def runB():
                for f_ in pendB:
                    f_()
                del pendB[:]

            def ldx(g, s):
                tok0 = g * 512 + s * 128
                xt, xk = xts.next()
                dma("sp", xt, x[tok0:tok0 + 128, :], [], [xk], "pm_x%d" % xk[1])
                xpre[(g, s)] = (xt, xk)

            def flush_store():
                for (g, h23, h2k) in pend_store:
                    dma("sp", h2T[:, :, g * 512:(g + 1) * 512], h23, [(h2k, s) for s in range(4)], [("h2T", g)], "pm_sh%d" % h2k[1])
                del pend_store[:]

            step1(0)
            for g in range(16):
                runB()
                if g + 1 < 16:
                    step1(g + 1)
                flush_m()
                flush_store()
                step2(g)
            runB()
            flush_m()
            flush_store()
            P.barrier()

        def phaseFFN():
            A.reset()
            W1 = A.b16(8 * DFF).rearrange("p (k n) -> p k n", n=DFF)
            W2 = A.b16(32 * D).rearrange("p (k n) -> p k n", n=D)
            w1k = load_w(W1, w_ff1, 8, "pfW1", "pf_w1")
            w2k = load_w(W2, w_ff2, 32, "pfW2", "pf_w2")
            PG2 = A.f32(D)
            load_rows(PG2, 5, "PG2", "pf_r0")
            NT = 256
            hts = Slots("pfht", [A.b16(8 * NT) for _ in range(2)])
            uT = A.b16(32 * NT)
            uT3 = uT.rearrange("p (f t) -> p f t", t=NT)
            rls = Slots("pfrl", [A.b16(NT) for _ in range(3)])
            xns = Slots("pfxn", [A.f32(D) for _ in range(3)])
            zs = Slots("pfz", [A.f32(D) for _ in range(2)])
            junk = A.b16(D)
            sqs = Slots("pfsq", [A.f32(2) for _ in range(4)])
            def ldh(g):
                ht, hk = hts.next()
                h3 = ht.rearrange("p (k t) -> p k t", t=NT)
                dma("sp", h3, h2T[:, :, g * NT:(g + 1) * NT], [("h2T", (g * NT) // 512)], [hk], "pf_h%d" % hk[1])
                return h3, hk

            xpre = {}

            def ldx(tok0):
                xn, xnk = xns.next()
                dma("sp", xn, xnew[tok0:tok0 + 128, :], [("xnew", tok0 // 128)], [xnk], "pf_x%d" % xnk[1])
                xpre[tok0] = (xn, xnk)

            nxt = ldh(0)
            for g in range(T // NT):
                h3, hk = nxt
                nxt = ldh(g + 1) if g + 1 < T // NT else None
                for fc in range(32):
                    bA, kA, _ = next_bank()
                    for k in range(8):
                        mm(bA[:, 0:NT], W1[:, k, fc * 128:(fc + 1) * 128], h3[:, k, :], k == 0, k == 7, [hk] + w1k, [kA])
                    rl, rlk = rls.next()
                    act(rl, bA[:, 0:NT], AF.Relu, [kA], [rlk])
                    tt("dve" if fc % 2 == 0 else "pool", uT3[:, fc, :], rl, rl, ALU.mult, [rlk], [("pfuT", fc)])
                for s in range(NT // 128):
                    tok0 = g * NT + s * 128
                    if tok0 not in xpre:
                        ldx(tok0)
                    xn, xnk = xpre.pop(tok0)
                    if tok0 + 128 < T:
                        ldx(tok0 + 128)
                    pr, pk, _ = next_pair()
                    for half in range(2):
                        for fc in range(32):
                            mm(pr[:, half * 512:(half + 1) * 512], uT3[:, fc, s * 128:(s + 1) * 128],
                               W2[:, fc, half * 512:(half + 1) * 512], fc == 0, fc == 31, [("pfuT", fc)] + w2k, [pk[half]])
                    sq, sqk = sqs.next()
                    act(junk, pr, AF.Square, pk, ["pfjunk", sqk], accum_out=sq[:, 0:1])
                    rstd_from_ssq(sq[:, 0:1], 1, 1.0 / D, [sqk], [sqk])
                    zt, zk = zs.next()
                    stt(zt, pr, sq[:, 0:1], PG2, ALU.mult, ALU.mult, pk + [sqk, "PG2"], [(zk, "a")])
                    tt("pool", zt, zt, xn, ALU.add, [(zk, "a"), xnk], [(zk, "b")])
                    dma("sp", out[tok0:tok0 + 128, :], zt, [(zk, "b")], [("out", tok0 // 128)], "pf_so%d" % zk[1])
            P.barrier()

        phase0()
        phaseH()
        phase_rope(0, qT, False, "pq")
        phase_rope(1024, kT, True, "pk")
        phaseV()
        phaseG()
        phaseGLAprep()
        phaseAttn()
        phaseGLA()
        phaseMerge()
        phaseFFN()
        P.add("sp", lambda e: e.nop())
        P.finalize(block, lambda name: st.enter_context(nc.semaphore(name)))
        build_program.stats = (len(P.ops), P.n_dma_sems)
    return nc


DEBUG_OUT = set()


def _rope_tables():
    t = np.arange(T)
    inv = (np.float32(10000.0) ** (-(np.arange(0, 32, 2, dtype=np.float32)) / np.float32(32))).astype(np.float32)
    ang_r = (t // 64).astype(np.float32)[:, None] * inv[None, :]
    ang_c = (t % 64).astype(np.float32)[:, None] * inv[None, :]
    cos = np.zeros((128, T), np.float32)
    sin = np.zeros((128, T), np.float32)
    for c in range(2):
        for j in range(64):
            ang = ang_r if j < 32 else ang_c
            f = j % 16
            first = (j % 32) < 16
            cos[c * 64 + j] = np.cos(ang[:, f])
            sin[c * 64 + j] = (-np.sin(ang[:, f])) if first else np.sin(ang[:, f])
    return cos, sin


def _swap_perm():
    perm = np.zeros(1024, np.int64)
    for hc in range(16):
        for j in range(64):
            pj = j + 16 if (j % 32) < 16 else j - 16
            perm[hc * 64 + j] = hc * 64 + pj
    return perm


def make_in_maps(inputs):
    f = lambda a: np.ascontiguousarray(np.asarray(a, dtype=np.float32))
    x = f(inputs["x"]); c = f(inputs["c"]); ctx = f(inputs["ctx"]); c_ctx = f(inputs["c_ctx"])
    w_in = f(inputs["w_in"][0])
    wgu = np.ascontiguousarray(np.concatenate([f(inputs["w_gate_up"][0]), f(inputs["b_gate_up"][0])[:, None, :]], axis=1))
    lambdas = np.ascontiguousarray(np.concatenate([f(inputs["lambda_q1"][0]), f(inputs["lambda_k1"][0]),
                                                   f(inputs["lambda_q2"][0]), f(inputs["lambda_k2"][0])])[None, :])
    norms = np.ascontiguousarray(np.stack([f(inputs["pre_norm1"][0]), f(inputs["post_norm1"][0]),
                                           f(inputs["pre_norm2"][0]), f(inputs["post_norm2"][0])]))
    cos, sin = _rope_tables()
    shared = {
        "w_mod": f(inputs["w_mod"][0]), "b_mod": f(inputs["b_mod"][0])[None, :], "norms": norms,
        "w_in": w_in, "wgu": wgu, "lambdas": lambdas,
        "hn_da": f(inputs["da_head_norm"][0])[None, :], "hn_gla": f(inputs["gla_head_norm"][0])[None, :],
        "w_bda": f(inputs["w_branch_da"][0]), "w_bgla": f(inputs["w_branch_gla"][0]), "w_out": f(inputs["w_out"][0]),
        "w_ff1": f(inputs["w_ff1"][0]), "w_ff2": f(inputs["w_ff2"][0]), "ropecos": cos, "ropesin": sin,
    }
    maps = []
    for b in range(NCORES):
        m = dict(shared)
        m["x"] = x[b]
        m["ctx"] = ctx[b]
        m["cvec"] = np.ascontiguousarray(np.concatenate([c[b].reshape(8, 128), c_ctx.reshape(8, 128)], axis=0))
        maps.append(m)
    return maps


def kernel(**inputs):
    nc = build_program()
    maps = make_in_maps(inputs)
    res = run_bass_kernel_spmd(nc, maps, core_ids=list(range(NCORES)))
    return np.stack([np.asarray(r["out"], dtype=np.float32) for r in res.results], axis=0)
```
